# Optimizing a Trainium2 kernel written in Bass

```python
import jax, jax.numpy as jnp
from jax import lax
import numpy as np

D_MODEL = 1024
BATCH = 8
SEQ = 2048
DEPTH = 2
DEC_BATCH = 128
DEC_SEQ = 4
PAST_LEN = 16384
PAGE_SIZE = 128

CONV_HEADS = 4
CONV_WIDTH = D_MODEL // 4
CONV_K = 3
HG_DK = 128
HG_DV = 128
HG_HEADS = (D_MODEL // 2) // HG_DV
HG_KDIM = HG_HEADS * HG_DK
HG_WIDTH = HG_HEADS * HG_DV
HG_CHUNK = 64
POOL_WINDOWS = (2, 4, 8, 16)
POOL_WIDTH = D_MODEL // 4
POOL_GROUP = POOL_WIDTH // len(POOL_WINDOWS)
POOL_BUF = max(POOL_WINDOWS) - 1

D_MIX = CONV_WIDTH + HG_WIDTH + POOL_WIDTH
IN_SIZES = (CONV_WIDTH, CONV_WIDTH, CONV_WIDTH, CONV_WIDTH,
            HG_KDIM, HG_KDIM, HG_WIDTH, HG_WIDTH,
            POOL_WIDTH, POOL_WIDTH)
D_IN = sum(IN_SIZES)
SPLIT_IDX = tuple(int(s) for s in np.cumsum(IN_SIZES)[:-1])
EPS = 1e-6

kernel_name = "hymba_conv_hgrn2_pool_decode_step"


def _rmsnorm(x, g):
    xf = x.astype(jnp.float32)
    y = xf * lax.rsqrt(jnp.mean(xf * xf, axis=-1, keepdims=True) + EPS)
    return (y * g.astype(jnp.float32)).astype(x.dtype)


def _short_conv(a_h, a_b, a_c, a_g, w, buf):
    T = a_h.shape[1]
    u = a_c * a_h
    full = jnp.concatenate([buf.astype(u.dtype), u], axis=1)
    y = full[:, 0:T] * w[0]
    for j in range(1, CONV_K):
        y = y + full[:, j:j + T] * w[j]
    out = a_b * y * jax.nn.silu(a_g)
    return out.astype(a_h.dtype), full[:, T:]


def _hgrn2(q, f_logit, i_in, g, lb, norm_g, s0):
    B, T, _ = q.shape
    f32 = jnp.float32
    zf = f_logit.astype(f32)
    lbf = lb.astype(f32)
    logf = jnp.logaddexp(jnp.log(lbf), jnp.log1p(-lbf) + jax.nn.log_sigmoid(zf))
    k = (1.0 - lbf) * jax.nn.sigmoid(-zf)
    qh = q.astype(f32).reshape(B, T, HG_HEADS, HG_DK)
    kh = k.reshape(B, T, HG_HEADS, HG_DK)
    gh = logf.reshape(B, T, HG_HEADS, HG_DK)
    vh = i_in.astype(f32).reshape(B, T, HG_HEADS, HG_DV)
    L = min(HG_CHUNK, T)
    n = -(-T // L)
    pad = n * L - T

    def to_chunks(a):
        a = jnp.pad(a, ((0, 0), (0, pad), (0, 0), (0, 0)))
        return a.reshape(B, n, L, HG_HEADS, a.shape[-1]).transpose(1, 0, 3, 2, 4)

    qc, kc, gc, vc = to_chunks(qh), to_chunks(kh), to_chunks(gh), to_chunks(vh)
    mask = jnp.tril(jnp.ones((L, L), dtype=bool))[None, None, :, :, None]

    def step(S, inp):
        qb, kb, gb, vb = inp
        b = jnp.cumsum(gb, axis=2)
        o_inter = jnp.einsum('bhld,bhde->bhle', qb * jnp.exp(b), S)
        diff = b[:, :, :, None, :] - b[:, :, None, :, :]
        decay = jnp.exp(jnp.where(mask, diff, -jnp.inf))
        A = jnp.einsum('bhtd,bhsd,bhtsd->bhts', qb, kb, decay)
        o_intra = jnp.einsum('bhts,bhse->bhte', A, vb)
        bL = b[:, :, -1]
        S_new = jnp.exp(bL)[..., None] * S + jnp.einsum(
            'bhsd,bhse->bhde', kb * jnp.exp(bL[:, :, None, :] - b), vb)
        return S_new, o_inter + o_intra

    S_fin, o = lax.scan(step, s0.astype(f32), (qc, kc, gc, vc))
    o = o.transpose(1, 0, 3, 2, 4).reshape(B, n * L, HG_HEADS, HG_DV)[:, :T]
    o = o * lax.rsqrt(jnp.mean(o * o, axis=-1, keepdims=True) + EPS)
    o = o.reshape(B, T, HG_WIDTH) * norm_g.astype(f32)
    out = o * jax.nn.silu(g.astype(f32))
    return out.astype(q.dtype), S_fin.astype(s0.dtype)


def _pool_mixer(v, gate, buf, start, pool_w, pool_scale):
    B, T, C = v.shape
    f32 = jnp.float32
    full = jnp.concatenate([buf.astype(v.dtype), v], axis=1)
    fullf = full.astype(f32)
    csum = jnp.concatenate([jnp.zeros((B, 1, C), f32), jnp.cumsum(fullf, axis=1)], axis=1)
    pos = start + jnp.arange(T)
    means = []
    for gi, w in enumerate(POOL_WINDOWS):
        sl = slice(gi * POOL_GROUP, (gi + 1) * POOL_GROUP)
        hi = csum[:, POOL_BUF + 1:POOL_BUF + 1 + T, sl]
        lo = csum[:, POOL_BUF + 1 - w:POOL_BUF + 1 - w + T, sl]
        cnt = jnp.minimum(w, pos + 1).astype(f32)[None, :, None]
        means.append((hi - lo) / cnt)
    u = jnp.concatenate(means, axis=-1) - v.astype(f32)
    u = u.reshape(B, T, len(POOL_WINDOWS), POOL_GROUP)
    y = jnp.einsum('btgc,gcd->btgd', u, pool_w.astype(f32)).reshape(B, T, C)
    y = y * pool_scale.astype(f32) * jax.nn.silu(gate.astype(f32))
    return y.astype(v.dtype), full[:, T:]


def _trunk(x, start, conv_bufs, hg_states, pool_bufs, norm_g, w_in, conv_w, hgrn_lb,
           hgrn_norm_g, pool_w, pool_scale, w_out, final_norm_g):
    lb_all = jnp.cumsum(jax.nn.softmax(hgrn_lb.astype(jnp.float32), axis=0), axis=0)
    lb_all = lb_all - lb_all[0:1]
    new_conv, new_hg, new_pool = [], [], []
    for l in range(DEPTH):
        h = _rmsnorm(x, norm_g[l])
        z = jnp.einsum('btd,de->bte', h, w_in[l].astype(h.dtype))
        a_h, a_b, a_c, a_g, q, f, i_in, g_b, p_v, p_g = jnp.split(z, SPLIT_IDX, axis=-1)
        out_a, cb = _short_conv(a_h, a_b, a_c, a_g, conv_w[l].astype(z.dtype), conv_bufs[l])
        out_b, hs = _hgrn2(q, f, i_in, g_b, lb_all[l], hgrn_norm_g[l], hg_states[l])
        out_c, pb = _pool_mixer(p_v, p_g, pool_bufs[l], start, pool_w[l], pool_scale[l])
        mix = jnp.concatenate([out_a, out_b, out_c], axis=-1)
        x = x + jnp.einsum('bte,ed->btd', mix, w_out[l].astype(mix.dtype)).astype(x.dtype)
        new_conv.append(cb)
        new_hg.append(hs)
        new_pool.append(pb)
    y = _rmsnorm(x, final_norm_g)
    return y, jnp.stack(new_conv), jnp.stack(new_hg), jnp.stack(new_pool)


def setup_inputs(seed: int = 0) -> dict:
    key = jax.random.key(seed)
    ks = jax.random.split(key, 16)
    f32 = jnp.float32
    nrm = lambda k, s, sc: sc * jax.random.normal(k, s, f32)
    return {
        "x_prompt": nrm(ks[0], (BATCH, SEQ, D_MODEL), 1.0),
        "x_sample": nrm(ks[1], (DEC_BATCH, DEC_SEQ, D_MODEL), 1.0),
        "state_conv": nrm(ks[2], (DEPTH, DEC_BATCH, CONV_K - 1, CONV_WIDTH), 1.0),
        "state_hgrn": nrm(ks[3], (DEPTH, DEC_BATCH, HG_HEADS, HG_DK, HG_DV), 0.5),
        "state_pool": nrm(ks[4], (DEPTH, DEC_BATCH, POOL_BUF, POOL_WIDTH), 1.0),
        "norm_g": 1.0 + nrm(ks[5], (DEPTH, D_MODEL), 0.05),
        "w_in": nrm(ks[6], (DEPTH, D_MODEL, D_IN), D_MODEL ** -0.5),
        "conv_w": nrm(ks[7], (DEPTH, CONV_K, CONV_WIDTH), CONV_K ** -0.5),
        "hgrn_lb": nrm(ks[8], (DEPTH, HG_KDIM), 0.5),
        "hgrn_norm_g": 1.0 + nrm(ks[9], (DEPTH, HG_WIDTH), 0.05),
        "pool_w": nrm(ks[10], (DEPTH, len(POOL_WINDOWS), POOL_GROUP, POOL_GROUP), POOL_GROUP ** -0.5),
        "pool_scale": 1.0 + nrm(ks[11], (DEPTH, POOL_WIDTH), 0.1),
        "w_out": nrm(ks[12], (DEPTH, D_MIX, D_MODEL), D_MIX ** -0.5),
        "final_norm_g": 1.0 + nrm(ks[13], (D_MODEL,), 0.05),
    }


def reference(x_prompt, x_sample, state_conv, state_hgrn, state_pool, norm_g, w_in, conv_w,
              hgrn_lb, hgrn_norm_g, pool_w, pool_scale, w_out, final_norm_g):
    Bp = x_prompt.shape[0]
    dt = x_prompt.dtype
    zc = jnp.zeros((DEPTH, Bp, CONV_K - 1, CONV_WIDTH), dt)
    zh = jnp.zeros((DEPTH, Bp, HG_HEADS, HG_DK, HG_DV), state_hgrn.dtype)
    zp = jnp.zeros((DEPTH, Bp, POOL_BUF, POOL_WIDTH), dt)
    y_prompt, new_conv_p, new_hgrn_p, new_pool_p = _trunk(
        x_prompt, 0, zc, zh, zp, norm_g, w_in, conv_w, hgrn_lb, hgrn_norm_g,
        pool_w, pool_scale, w_out, final_norm_g)
    y_sample, new_conv_s, new_hgrn_s, new_pool_s = _trunk(
        x_sample, PAST_LEN, state_conv, state_hgrn, state_pool, norm_g, w_in, conv_w,
        hgrn_lb, hgrn_norm_g, pool_w, pool_scale, w_out, final_norm_g)
    return (y_prompt, y_sample, new_conv_p, new_hgrn_p, new_pool_p,
            new_conv_s, new_hgrn_s, new_pool_s)
```

```python
import numpy as np
from collections import defaultdict
from contextlib import ExitStack
import concourse.bass as bass
import concourse.mybir as mybir
from concourse.bass_utils import run_bass_kernel_spmd

F32 = mybir.dt.float32
BF16 = mybir.dt.bfloat16
AF = mybir.ActivationFunctionType
ALU = mybir.AluOpType

D = 1024
DIN = 3584
EPS = 1e-6
NCORES = 8
TB = 256
LP = 64
NSEQ_S = 16
TS = 4
NSAMP = NSEQ_S * TS
import os
SAME_ENG_SYNC = not bool(os.environ.get("KNOSAME"))
SUB = int(os.environ.get('KSUB', '9'))

C_AH, C_AB, C_AC, C_AG = 0, 256, 512, 768
C_Q, C_F, C_I, C_GB = 1024, 1536, 2048, 2560
C_PV, C_PG = 3072, 3328

CF_IDENT = 0
CF_MASKP = 128
CF_MASKS = 256
CF_SCANP = 320
CF_SCANS = 576
CF_SELS = 640
CF_INVW = 656
CF_RCFIX = 658
NCF = 688


def _make_consts():
    cf = np.zeros((128, NCF), np.float32)
    cf[:, CF_IDENT:CF_IDENT + 128] = np.eye(128, dtype=np.float32)
    s = np.arange(128)[:, None]
    t = np.arange(128)[None, :]
    cf[:, CF_MASKP:CF_MASKP + 128] = -((t >= s) & (s // LP == t // LP)).astype(np.float32)
    s2 = np.arange(64)[:, None]
    t2 = np.arange(64)[None, :]
    cf[:64, CF_MASKS:CF_MASKS + 64] = -((t2 >= s2) & (s2 // TS == t2 // TS)).astype(np.float32)
    cf[:, CF_SCANP:CF_SCANP + 256] = (np.arange(256) % LP != 0).astype(np.float32)[None, :]
    cf[:, CF_SCANS:CF_SCANS + 64] = (np.arange(64) % TS != 0).astype(np.float32)[None, :]
    cf[:64, CF_SELS:CF_SELS + 16] = (np.arange(64)[:, None] // TS == np.arange(16)[None, :]).astype(np.float32)
    p = np.arange(128)
    wins = np.stack([np.where(p < 64, 2.0, 4.0), np.where(p < 64, 8.0, 16.0)], axis=1)
    cf[:, CF_INVW:CF_INVW + 2] = 1.0 / wins
    tt = np.arange(15)[None, None, :] + 1.0
    cf[:, CF_RCFIX:CF_RCFIX + 30] = (1.0 / np.minimum(wins[:, :, None], tt)).reshape(128, 30)
    return cf


class Buf:
    __slots__ = ("name", "w", "r", "rd", "gen", "excl")

    def __init__(self, name, excl=False):
        self.name = name
        self.excl = excl
        self.w = None
        self.r = {}
        self.rd = []
        self.gen = 0


class BH:
    __slots__ = ("buf", "gen", "holder")

    def __init__(self, buf, holder=None):
        holder = buf if holder is None else holder
        holder.gen += 1
        self.holder = holder
        self.buf = buf
        self.gen = holder.gen


class Sched:
    ENGS = ("pe", "act", "dve", "pool", "sp")

    def __init__(self, nc, es):
        self.nc = nc
        self.es = es
        self.engs = {"pe": nc.tensor, "act": nc.scalar, "dve": nc.vector, "pool": nc.gpsimd, "sp": nc.sync}
        self.ops = []
        self.dma_sems = {}
        self.oplog = [] if os.environ.get("KOPLOG") else None
        self.tags = []
        self.tag = ""

    def op(self, eng, fn, kw, reads=(), writes=(), dma_key=None):
        oid = len(self.ops)
        waits = set()
        reads = [self._chk(b) for b in reads]
        writes = [self._chk(b) for b in writes]
        writes = writes + [b for b in reads if b.excl]
        reads = [b for b in reads if not b.excl]
        cand = set()
        for b in reads:
            if b.w is not None:
                cand.add((b.w, False))
        for b in writes:
            if b.w is not None:
                cand.add((b.w, b.excl))
            for x in b.r.values():
                cand.add((x, b.excl))
            for x in b.rd:
                cand.add((x, b.excl))
        for (w, ex) in cand:
            we, wkey = self.ops[w][0], self.ops[w][4]
            if wkey is None and we == eng and dma_key is None:
                if eng == "pe" or ex or not SAME_ENG_SYNC:
                    continue
            waits.add(w)
        waits.discard(oid)
        for w in waits:
            self.ops[w][3] = True
        self.ops.append([eng, (fn, kw), waits, False, dma_key])
        self.tags.append(self.tag)
        for b in reads:
            if dma_key is not None:
                b.rd.append(oid)
            else:
                b.r[eng] = oid
        for b in writes:
            b.w = oid
            b.r = {}
            b.rd = []
        return oid

    @staticmethod
    def _chk(b):
        if isinstance(b, BH):
            assert b.gen == b.holder.gen, f"use of ring slot {b.holder.name} after reallocation"
            return b.buf
        return b

    def dma(self, queue, out, in_, reads, writes, key):
        eng = self.engs[queue]
        return self.op(queue, eng.dma_start, dict(out=out, in_=in_), reads, writes, dma_key=key)

    def flush(self):
        nc, es = self.nc, self.es
        esem = {e: es.enter_context(nc.semaphore("sem_" + e)) for e in self.ENGS}
        cnt = defaultdict(int)
        semobj = {}
        waited = {e: defaultdict(int) for e in self.ENGS}
        comp = [None] * len(self.ops)
        for oid, (e, fn, waits, signal, dma_key) in enumerate(self.ops):
            eng = self.engs[e]
            need = {}
            for w in waits:
                sem, val = comp[w]
                if need.get(sem.num, (None, 0))[1] < val:
                    need[sem.num] = (sem, val)
            dbg = []
            for num, (sem, val) in need.items():
                if waited[e][num] < val:
                    eng.wait_ge(sem, val)
                    waited[e][num] = val
                    dbg.append((sem.name, val))
            if os.environ.get("KDBG"):
                print(oid, e, getattr(fn[0], "__name__", "?"), "waits", dbg, "signal", signal, "dma", dma_key)
            ins = fn[0](**fn[1])
            if self.oplog is not None:
                try:
                    self.oplog.append((oid, e, getattr(fn[0], "__name__", "?"), ins.ins.name, sorted(waits), dma_key, self.tags[oid]))
                except Exception:
                    pass
            if dma_key is not None:
                if dma_key not in self.dma_sems:
                    self.dma_sems[dma_key] = es.enter_context(nc.semaphore("dq_" + dma_key))
                sem = self.dma_sems[dma_key]
                cnt[sem.num] += 16
                semobj[sem.num] = sem
                ins.then_inc(sem, 16)
                comp[oid] = (sem, cnt[sem.num])
            elif signal:
                sem = esem[e]
                cnt[sem.num] += 1
                ins.then_inc(sem, 1)
                comp[oid] = (sem, cnt[sem.num])
        for num, sem in semobj.items():
            nc.sync.wait_ge(sem, cnt[num])


class Ring:
    def __init__(self, tensor, nslots, slot_elems, name):
        self.t = tensor
        self.n = nslots
        self.se = slot_elems
        self.bufs = [Buf(f"{name}{i}") for i in range(nslots)]
        self.pos = 0

    def alloc(self, k=1):
        if self.pos + k > self.n:
            self.pos = 0
        i = self.pos
        self.pos += k
        ap = self.t[:, i * self.se:(i + k) * self.se]
        return ap, [BH(b) for b in self.bufs[i:i + k]]


class _Stop(Exception):
    pass


def build(nblk=8, stage=99):
    NP = nblk * TB
    nc = bass.Bass("TRN2", target_bir_lowering=False)
    dt_in = lambda n, s: nc.dram_tensor(n, s, F32, kind="ExternalInput").ap()
    dt_out = lambda n, s: nc.dram_tensor(n, s, F32, kind="ExternalOutput").ap()
    xp = dt_in("xp", [NP, D])
    xs = dt_in("xs", [NSAMP, D])
    sconv = dt_in("sconv", [2, 32, 256])
    shg = dt_in("shg", [2, NSEQ_S, 4, 128, 128])
    spool = dt_in("spool", [2, 240, 256])
    w_in = dt_in("w_in", [2, D, DIN])
    w_out = dt_in("w_out", [2, D, D])
    pp = dt_in("pp", [64, 128])
    pw = dt_in("pw", [2, 4, 64, 64])
    gfin_d = dt_in("gfin", [1, D])
    cf_d = dt_in("cf", [128, NCF])
    yp = dt_out("yp", [NP, D])
    ys = dt_out("ys", [NSAMP, D])
    ncp = dt_out("ncp", [2, 2, 256])
    nhp = dt_out("nhp", [2, 4, 128, 128])
    npp = dt_out("npp", [2, 15, 256])
    ncs = dt_out("ncs", [2, 32, 256])
    nhs = dt_out("nhs", [2, NSEQ_S, 4, 128, 128])
    nps = dt_out("nps", [2, 240, 256])

    es = ExitStack()
    with es:
        S = Sched(nc, es)
        sb = lambda n, s, d=F32: es.enter_context(nc.sbuf_tensor(n, s, d))
        ntile_p = NP // 128
        NT = ntile_p + 1
        xres = sb("xres", [128, NT, D])
        xres_b = [Buf(f"xres{i}") for i in range(NT)]
        win = [sb(f"win{g}", [128, 8, 512], BF16) for g in range(7)]
        win_b = [Buf(f"win{g}") for g in range(7)]
        wout = sb("wout", [128, 8, D], BF16)
        wout_b = Buf("wout")
        hT = sb("hT", [128, 8, TB], BF16)
        hT_b = Buf("hT")
        mixT = sb("mixT", [128, 8, TB], BF16)
        mix_b = [Buf(f"mix{k}") for k in range(8)]
        hbf = sb("hbf", [128, D], BF16)
        hbf_b = Buf("hbf")
        gfin = sb("gfin_sb", [128, D])
        gfin_b = Buf("gfin")
        cf = sb("cf_sb", [128, NCF])
        cf_b = Buf("cf")
        cb = sb("cb_sb", [128, 256], BF16)
        cb_b = Buf("cb")
        ident_bf = cb[:, 0:128]
        ones_bf = cb[:, 128:256]
        ppt = sb("ppt", [64, 128])
        ppt_b = Buf("ppt")
        pcol = sb("pcol", [128, 64])
        pcol_b = Buf("pcol")
        lbcol = sb("lbcol", [128, 8])
        lb_b = Buf("lbcol")
        small = sb("small", [128, 16])
        small_b = Buf("small")
        pwf = sb("pwf", [128, 2, 128])
        pwf_b = Buf("pwf")
        pwb = sb("pwb", [128, 2, 128], BF16)
        pwb_b = Buf("pwb")
        ubuf = [sb(f"ubuf{c}", [128, 260]) for c in range(2)]
        ubuf_b = [Buf(f"ubuf{c}") for c in range(2)]
        vbuf = [sb(f"vbuf{c}", [128, 304]) for c in range(2)]
        vbuf_b = [Buf(f"vbuf{c}") for c in range(2)]
        s32 = sb("s32", [128, 4, 128])
        s32_b = [Buf(f"s32_{h}") for h in range(4)]
        sbp = sb("sbp", [128, 4, 2, 128], BF16)
        sbp_b = [Buf(f"sbp{i}") for i in range(2)]
        s0st = sb("s0st", [128, 4, 4, 128])
        s0st_b = [Buf(f"s0st{i}") for i in range(4)]
        ebl = sb("ebl", [128, 4, 16])
        ebl_b = Buf("ebl")
        R32 = 19
        R16 = 20
        r32t = sb("r32", [128, R32 * 256])
        r16t = sb("r16", [128, R16 * 256], BF16)
        r32 = Ring(r32t, R32, 256, "r32_")
        r16 = Ring(r16t, R16, 256, "r16_")
        pst = [es.enter_context(nc.psum_tensor(f"psb{i}", [128, 512], F32)) for i in range(7)]
        psbf = es.enter_context(nc.psum_tensor("psbf", [128, 1024], BF16))
        bank_b = [Buf(f"psbank{i}", excl=True) for i in range(8)]
        ph_slots = []
        for j in range(2):
            for i in range(5):
                ph_slots.append((pst[i][:, j * 256:(j + 1) * 256], bank_b[i]))
        ph_pos = [0]

        ph_hold = [Buf(f"phslot{i}") for i in range(10)]

        def ph():
            i = ph_pos[0]
            ph_pos[0] = (i + 1) % len(ph_slots)
            return ph_slots[i][0], BH(ph_slots[i][1], ph_hold[i])

        pf_slots = [(pst[5], bank_b[5]), (pst[6], bank_b[6])]
        pf_pos = [0]

        def pf():
            i = pf_pos[0]
            pf_pos[0] = (i + 1) % 2
            return pf_slots[i]

        def pbf():
            return psbf, bank_b[7]

        V, A, G, PE = nc.vector, nc.scalar, nc.gpsimd, nc.tensor

        def dve(name, reads, writes, **kw):
            S.op("dve", getattr(V, name), kw, reads, writes)

        def act(reads, writes, **kw):
            S.op("act", A.activation, kw, reads, writes)

        def pool(name, reads, writes, **kw):
            S.op("pool", getattr(G, name), kw, reads, writes)

        def mm(reads, writes, **kw):
            S.op("pe", PE.matmul, kw, reads, writes)

        def tp(reads, writes, **kw):
            S.op("pe", PE.transpose, kw, reads, writes)

        vtok = sb("vtok", [128, 2, 512], BF16)
        vtok_b = Buf("vtok")

        S.dma("sp", cf[:], cf_d[:, :], [], [cf_b], "cf")
        S.dma("sp", ppt[:], pp[:, :], [], [ppt_b], "ppt")
        S.dma("sp", gfin[:], gfin_d.partition_broadcast(128), [], [gfin_b], "gfin")
        ident_f = cf[:, CF_IDENT:CF_IDENT + 128]
        pool("tensor_copy", [cf_b], [cb_b], out=ident_bf, in_=ident_f)
        pool("memset", [], [cb_b], ap=ones_bf, constant=1.0)
        pa, pab = ph()
        tp([ppt_b, cf_b], [pab], out=pa[:, 0:64], in_=ppt[:, :], identity=ident_f[0:64, 0:64])
        dve("tensor_copy", [pab], [pcol_b], out=pcol[:], in_=pa[:, 0:64])
        dve("tensor_tensor", [pcol_b], [small_b], out=small[:, 0:4], in0=pcol[:, 6:10], in1=pcol[:, 30:34], op=ALU.subtract)
        act([small_b], [small_b], out=small[:, 4:8], in_=small[:, 0:4], func=AF.Exp)
        dve("tensor_scalar", [small_b], [small_b], out=small[:, 8:12], in0=small[:, 4:8], scalar1=1.0, scalar2=None, op0=ALU.add)
        dve("reciprocal", [small_b], [lb_b], out=lbcol[:, 0:4], in_=small[:, 8:12])

        def load_wout(l):
            S.dma("pool", wout[:], w_out[l].rearrange("(k p) c -> p k c", p=128), [], [wout_b], "wout")

        reloaded = set()

        def load_group(l, g):
            wv = w_in[l].rearrange("(k p) c -> p k c", p=128)
            S.dma("pool", win[g][:], wv[:, :, g * 512:(g + 1) * 512], [], [win_b[g]], f"win{g}")
            reloaded.add((l, g))

        def load_weights(l, with_wout=True):
            pool("memset", [], [pwf_b], ap=pwf[:], constant=0.0)
            for g in (3, 0, 1, 2, 4, 6, 5):
                if (l, g) not in reloaded:
                    load_group(l, g)
            if with_wout:
                load_wout(l)
            for g in range(4):
                ct, hf = g // 2, g % 2
                S.dma("sp", pwf[hf * 64:(hf + 1) * 64, ct, hf * 64:(hf + 1) * 64], pw[l, g], [], [pwf_b], "pwf")
            pool("tensor_copy", [pwf_b], [pwb_b], out=pwb[:], in_=pwf[:])

        dmy = sb("dmy", [128, 4])
        dmy_b = Buf("dmy")
        S.op("pool", G.memset, dict(ap=dmy[:], constant=1.0), [], [dmy_b])

        def warm_act_table(func):
            act([dmy_b], [dmy_b], out=dmy[:, 0:1], in_=dmy[:, 1:2], func=func)

        smt = sb("smt", [128, 32])
        smt_b = [Buf(f"smt{i}") for i in range(8)]
        sm_pos = [0]
        pre_rstd = {}

        def rstd_early(l, ti, P):
            pre_rstd[(l, ti)] = rstd_of(xres[0:P, ti, :], xres_b[ti], P)

        def rstd_of(xt, xb, P, junk=None):
            k_ = sm_pos[0]
            sm_pos[0] = (k_ + 1) % 8
            sm, smb = smt[:, 4 * k_:4 * k_ + 4], [smt_b[k_]]
            if junk is None:
                junk_ap, junk_b = hbf[0:P, :], [hbf_b]
            else:
                junk_ap, junk_b = junk[0][0:P, :], junk[1]
            act([xb], junk_b + [smb[0]], out=junk_ap, in_=xt, func=AF.Square, accum_out=sm[0:P, 0:1])
            act(smb, smb, out=sm[0:P, 1:2], in_=sm[0:P, 0:1], func=AF.Ln, scale=1.0 / D, bias=EPS)
            act(smb, smb, out=sm[0:P, 2:3], in_=sm[0:P, 1:2], func=AF.Exp, scale=-0.5)
            return sm[0:P, 2:3], smb

        def norm_parts(l, ti, P, tok0):
            xt = xres[0:P, ti, :]
            xb = xres_b[ti]
            if (l, ti) in pre_rstd:
                rstd, smb = pre_rstd.pop((l, ti))
            else:
                rstd, smb = rstd_of(xt, xb, P)
            act([xb] + smb, [hbf_b], out=hbf[0:P, :], in_=xt, func=AF.Copy, scale=rstd)
            gc = pcol[:, 24 * l + 16:24 * l + 24]
            st_ = {}

            def pe_fn():
                pt, ptb = pbf()
                ptv = pt.rearrange("p (k t) -> p k t", t=128)
                for kc in range(8):
                    tp([hbf_b, cb_b], [ptb], out=ptv[:, kc, 0:P], in_=hbf[0:P, kc * 128:(kc + 1) * 128], identity=ident_bf[0:P, 0:P])
                st_.update(ptv=ptv, ptb=ptb)

            def evac_fn():
                dve("tensor_tensor", [st_["ptb"], pcol_b], [hT_b], out=hT[:, :, tok0:tok0 + P], in0=st_["ptv"][:, :, 0:P],
                    in1=gc.unsqueeze(2).to_broadcast([128, 8, P]), op=ALU.mult)

            return pe_fn, evac_fn

        def norm_transpose(l, ti, P, tok0):
            pe_fn, evac_fn = norm_parts(l, ti, P, tok0)
            pe_fn()
            evac_fn()

        def proj_fm(col, T):
            g, off = col // 512, col % 512
            ps, psb = ph()
            for kc in range(8):
                mm([win_b[g], hT_b], [psb], out=ps[:, 0:T], lhsT=win[g][:, kc, off:off + 128], rhs=hT[:, kc, 0:T],
                   start=(kc == 0), stop=(kc == 7))
            return ps, psb

        def hg_front1(l, T):
            HR = range(4)
            FS = {}
            pfs = [proj_fm(C_F + h * 128, T) for h in HR]
            A4, A4b = r32.alloc(4)
            B4, B4b = r32.alloc(4)
            As = [(A4[:, h * 256:(h + 1) * 256], [A4b[h]]) for h in HR]
            Bs = [(B4[:, h * 256:(h + 1) * 256], [B4b[h]]) for h in HR]
            A4v = A4.rearrange("p (h t) -> p h t", h=4)[:, :, 0:T]
            B4v = B4.rearrange("p (h t) -> p h t", h=4)[:, :, 0:T]
            for h in HR:
                act([pfs[h][1]], As[h][1], out=As[h][0][:, 0:T], in_=pfs[h][0][:, 0:T], func=AF.Exp, scale=-1.0)
            act(A4b, B4b, out=B4v, in_=A4v, func=AF.Ln, bias=1.0)
            X4 = {"A": (A4v, A4b, A4), "B": (B4v, B4b, B4)}
            Ds = As if l == 0 else Bs
            FS.update(pfs=pfs, As=As, Bs=Bs, Cs=None, Ds=Ds, X4=X4, D4=X4["A" if l == 0 else "B"], F4=X4["B" if l == 0 else "A"])
            return FS


        def block(l, grp, bi, T, tiles, nseq, Tt, L, last, prenormed, nxt, pre, prev_wout):
            nch = T // L
            P0 = tiles[0][1]
            ntl = len(tiles)
            pc = 24 * l
            first = (bi == 0)
            if grp == "p":
                scanmask = cf[:, CF_SCANP:CF_SCANP + T]
                negmask = cf[0:P0, CF_MASKP:CF_MASKP + P0]
            else:
                scanmask = cf[:, CF_SCANS:CF_SCANS + T]
                negmask = cf[0:P0, CF_MASKS:CF_MASKS + P0]
            if stage < 1:
                raise _Stop()
            if not prenormed:
                tok0 = 0
                for (ti, P) in tiles:
                    norm_transpose(l, ti, P, tok0)
                    tok0 += P
            if stage < 2:
                raise _Stop()

            def v3(ap):
                return ap.rearrange("p (s t) -> p s t", s=nseq)

            nkh4, nkh4b = r16.alloc(4)
            nkb4, nkb4b = r16.alloc(4)
            nkhv = nkh4.rearrange("p (h t) -> p h t", h=4)
            nkbv = nkb4.rearrange("p (h t) -> p h t", h=4)
            HR = range(4)
            FS = {}
            deferred = []
            nxtFS = [None]
            late_wout = []
            prenorm_done = [False]

            def front2():
                pfs, As, Bs, Ds = FS['pfs'], FS['As'], FS['Bs'], FS['Ds']
                A4v, A4b, _ = FS['X4']["A"]
                B4v, B4b, _ = FS['X4']["B"]
                D4v, D4b, _ = FS['D4']
                F4v, F4b, _ = FS['F4']
                C4, C4b = r32.alloc(4)
                Cs = [(C4[:, h * 256:(h + 1) * 256], [C4b[h]]) for h in HR]
                C4v = C4.rearrange("p (h t) -> p h t", h=4)[:, :, 0:T]
                FS['Cs'] = Cs
                FS['C4'] = (C4v, C4b, C4)
                if l == 0:
                    for h in HR:
                        dve("tensor_tensor_scan", Bs[h][1] + [cf_b], Cs[h][1], out=Cs[h][0][:, 0:T], data0=scanmask, data1=Bs[h][0][:, 0:T],
                            initial=0.0, op0=ALU.mult, op1=ALU.subtract)
                    act(B4b, B4b, out=B4v, in_=B4v, func=AF.Exp, scale=-1.0)
                else:
                    for h in HR:
                        act(As[h][1] + [lb_b], As[h][1], out=As[h][0][:, 0:T], in_=As[h][0][:, 0:T], func=AF.Ln, scale=lbcol[:, h:h + 1],
                            bias=1.0)
                    dve("tensor_tensor", A4b + B4b, A4b, out=A4v, in0=A4v, in1=B4v, op=ALU.subtract)
                    for h in HR:
                        dve("tensor_tensor_scan", As[h][1] + [cf_b], Cs[h][1], out=Cs[h][0][:, 0:T], data0=scanmask, data1=As[h][0][:, 0:T],
                            initial=0.0, op0=ALU.mult, op1=ALU.add)
                    act(A4b, A4b, out=A4v, in_=A4v, func=AF.Exp)
                pqs = [proj_fm(C_Q + h * 128, T) for h in HR]
                act(C4b, D4b, out=D4v, in_=C4v, func=AF.Exp)
                act(C4b, C4b, out=C4v, in_=C4v, func=AF.Exp, scale=-1.0)
                FS.update(pqs=pqs)

            def front3():
                Ds, pqs = FS['Ds'], FS['pqs']
                D4v, D4b, D4 = FS['D4']
                F4v, F4b, F4 = FS['F4']
                C4v, C4b, C4 = FS['C4']
                qTs = []
                for h in HR:
                    qT, qTb = r16.alloc()
                    dve("tensor_tensor", [pqs[h][1]] + Ds[h][1], qTb, out=qT[:, 0:T], in0=pqs[h][0][:, 0:T], in1=Ds[h][0][:, 0:T], op=ALU.mult)
                    qTs.append((qT, qTb))
                dve("scalar_tensor_tensor", F4b + C4b, nkb4b, out=nkbv[:, :, 0:T], in0=F4v, scalar=1.0, in1=C4v,
                    op0=ALU.subtract, op1=ALU.mult)
                act(D4b, [ebl_b], out=ebl[:, :, 0:nch], in_=D4.rearrange("p (h t) -> p h t", h=4)[:, :, L - 1:T:L], func=AF.Copy)
                dve("tensor_tensor", nkb4b + [ebl_b], nkh4b, out=nkhv[:, :, 0:T].rearrange("p h (c l) -> p h c l", l=L),
                    in0=nkbv[:, :, 0:T].rearrange("p h (c l) -> p h c l", l=L),
                    in1=ebl[:, :, 0:nch].unsqueeze(3).to_broadcast([128, 4, nch, L]), op=ALU.mult)
                FS.update(qTs=qTs)

            def vproj():
                tok0 = 0
                for tix, (ti, P) in enumerate(tiles):
                    pv_, pvb_ = pf()
                    for kc in range(8):
                        mm([hT_b, win_b[4]], [pvb_], out=pv_[0:P, :], lhsT=hT[:, kc, tok0:tok0 + P], rhs=win[4][:, kc, :],
                           start=(kc == 0), stop=(kc == 7))
                    act([pvb_], [vtok_b], out=vtok[0:P, tix, :], in_=pv_[0:P, :], func=AF.Copy)
                    tok0 += P

            def convmix_both():
                W = Tt + 2
                CT = range(2)
                ubs = [ubuf[ct][:, 0:nseq * W].rearrange("p (s w) -> p s w", w=W) for ct in CT]
                cw = lambda k, ct: pcol[:, pc + 2 * k + ct:pc + 2 * k + ct + 1]
                if first:
                    if grp == "p":
                        for ct in CT:
                            pool("memset", [], [ubuf_b[ct]], ap=ubs[ct][:, :, 0:2], constant=0.0)
                    else:
                        scv, scvb = r32.alloc()
                        S.dma("sp", scv[0:32, :], sconv[l], [], scvb, "dq" + scvb[0].buf.name)
                        for ct in CT:
                            pa_, pab_ = ph()
                            tp(scvb + [cf_b], [pab_], out=pa_[:, 0:32], in_=scv[0:32, ct * 128:(ct + 1) * 128], identity=ident_f[0:32, 0:32])
                            act([pab_], [ubuf_b[ct]], out=ubs[ct][:, :, 0:2], in_=pa_[:, 0:32].rearrange("p (s j) -> p s j", j=2),
                                func=AF.Copy)
                pah = [proj_fm(C_AH + ct * 128, T) for ct in CT]
                pac = [proj_fm(C_AC + ct * 128, T) for ct in CT]
                ahs = [r32.alloc() for ct in CT]
                for ct in CT:
                    act([pah[ct][1]], ahs[ct][1], out=ahs[ct][0][:, 0:T], in_=pah[ct][0][:, 0:T], func=AF.Copy)
                pag = [proj_fm(C_AG + ct * 128, T) for ct in CT]
                for ct in CT:
                    dve("tensor_tensor", [pac[ct][1]] + ahs[ct][1], [ubuf_b[ct]], out=ubs[ct][:, :, 2:2 + Tt], in0=v3(pac[ct][0][:, 0:T]),
                        in1=v3(ahs[ct][0][:, 0:T]), op=ALU.mult)
                y1s = [r32.alloc() for ct in CT]
                for ct in CT:
                    act([ubuf_b[ct], pcol_b], y1s[ct][1], out=v3(y1s[ct][0][:, 0:T]), in_=ubs[ct][:, :, 0:Tt], func=AF.Copy, scale=cw(0, ct))
                pab2 = [proj_fm(C_AB + ct * 128, T) for ct in CT]
                for k in (1, 2):
                    for ct in CT:
                        dve("scalar_tensor_tensor", [ubuf_b[ct], pcol_b] + y1s[ct][1], y1s[ct][1], out=v3(y1s[ct][0][:, 0:T]),
                            in0=ubs[ct][:, :, k:k + Tt], scalar=cw(k, ct), in1=v3(y1s[ct][0][:, 0:T]), op0=ALU.mult, op1=ALU.add)
                sas = [r32.alloc() for ct in CT]
                for ct in CT:
                    act([pag[ct][1]], sas[ct][1], out=sas[ct][0][:, 0:T], in_=pag[ct][0][:, 0:T], func=AF.Silu)
                for ct in CT:
                    dve("tensor_tensor", [pab2[ct][1]] + y1s[ct][1], y1s[ct][1], out=y1s[ct][0][:, 0:T], in0=pab2[ct][0][:, 0:T],
                        in1=y1s[ct][0][:, 0:T], op=ALU.mult)
                warm_act_table(AF.Exp)
                FS["conv2"] = (CT, y1s, sas, ubs)

            def convmix_part2():
                CT, y1s, sas, ubs = FS["conv2"]
                for ct in CT:
                    dve("tensor_tensor", y1s[ct][1] + sas[ct][1], [mix_b[ct]], out=mixT[:, ct, 0:T], in0=y1s[ct][0][:, 0:T],
                        in1=sas[ct][0][:, 0:T], op=ALU.mult)
                for ct in CT:
                    if last:
                        deferred.append(lambda ct=ct: conv_state_out(ct, ubs[ct], ubuf_b[ct]))
                    else:
                        pool("tensor_copy", [ubuf_b[ct]], [ubuf_b[ct]], out=ubs[ct][:, :, 0:2], in_=ubs[ct][:, :, Tt:Tt + 2])

            def conv_state_out(ct, ub, ubb):
                if True:
                    rows = nseq * 2
                    pa_, pab_ = ph()
                    st, stb = r32.alloc()
                    tmpc, tmpcb = r32.alloc()
                    pool("tensor_copy", [ubb], tmpcb, out=tmpc[:, 0:rows].rearrange("p (s j) -> p s j", j=2), in_=ub[:, :, Tt:Tt + 2])
                    tp(tmpcb + [cf_b], [pab_], out=pa_[0:rows, 0:128], in_=tmpc[:, 0:rows], identity=ident_f)
                    act([pab_], stb, out=st[0:rows, 0:128], in_=pa_[0:rows, 0:128], func=AF.Copy)
                    dst = (ncp if grp == "p" else ncs)[l, :, ct * 128:(ct + 1) * 128]
                    S.dma("sp", dst, st[0:rows, 0:128], stb, [], "dq" + stb[0].buf.name)

            def poolmix_both():
                W = Tt + 15
                CT = range(2)
                vbs = [vbuf[ct][:, 0:nseq * W].rearrange("p (s w) -> p s w", w=W) for ct in CT]
                if first:
                    if grp == "p":
                        for ct in CT:
                            pool("memset", [], [vbuf_b[ct]], ap=vbs[ct][:, :, 0:15], constant=0.0)
                    else:
                        for g2 in range(2):
                            spl_, splb_ = r32.alloc()
                            S.dma("sp", spl_[0:120, :], spool[l, g2 * 120:(g2 + 1) * 120, :], [], splb_, "dq" + splb_[0].buf.name)
                            for ct in CT:
                                pa_, pab_ = ph()
                                tp(splb_ + [cf_b], [pab_], out=pa_[:, 0:120], in_=spl_[0:120, ct * 128:(ct + 1) * 128],
                                   identity=ident_f[0:120, 0:120])
                                act([pab_], [vbuf_b[ct]], out=vbs[ct][:, g2 * 8:(g2 + 1) * 8, 0:15],
                                    in_=pa_[:, 0:120].rearrange("p (s j) -> p s j", j=15), func=AF.Copy)
                ppv = [proj_fm(C_PV + ct * 128, T) for ct in CT]
                for ct in CT:
                    act([ppv[ct][1]], [vbuf_b[ct]], out=vbs[ct][:, :, 15:15 + Tt], in_=v3(ppv[ct][0][:, 0:T]), func=AF.Copy)
                ppg = [proj_fm(C_PG + ct * 128, T) for ct in CT]

                def warr():
                    a_, b_ = r32.alloc(2)
                    return a_[:, 0:nseq * W].rearrange("p (s w) -> p s w", w=W), b_

                s2 = [warr() for ct in CT]
                s4 = [warr() for ct in CT]
                for ct in CT:
                    dve("tensor_tensor", [vbuf_b[ct]], s2[ct][1], out=s2[ct][0][:, :, 1:W], in0=vbs[ct][:, :, 1:W], in1=vbs[ct][:, :, 0:W - 1],
                        op=ALU.add)
                a2, a2b = s2[0]
                sw0, sw0b = s4[0]
                dve("tensor_tensor", a2b, sw0b, out=sw0[64:128, :, 3:W], in0=a2[64:128, :, 3:W], in1=a2[64:128, :, 1:W - 2], op=ALU.add)
                b2, b2b = s2[1]
                b4, b4b = s4[1]
                dve("tensor_tensor", b2b, b4b, out=b4[:, :, 3:W], in0=b2[:, :, 3:W], in1=b2[:, :, 1:W - 2], op=ALU.add)
                act(a2b, sw0b, out=sw0[0:64, :, 15:W], in_=a2[0:64, :, 15:W], func=AF.Copy)
                b8, b8b = b2, b2b
                dve("tensor_tensor", b4b, b8b, out=b8[:, :, 7:W], in0=b4[:, :, 7:W], in1=b4[:, :, 3:W - 4], op=ALU.add)
                sw1, sw1b = b4, b4b
                dve("tensor_tensor", b8b, sw1b, out=sw1[64:128, :, 15:W], in0=b8[64:128, :, 15:W], in1=b8[64:128, :, 7:W - 8], op=ALU.add)
                act(b8b, sw1b, out=sw1[0:64, :, 15:W], in_=b8[0:64, :, 15:W], func=AF.Copy)
                sws = [(sw0, sw0b), (sw1, sw1b)]
                us = [r16.alloc() for ct in CT]
                for ct in CT:
                    sw, swb = sws[ct]
                    dve("scalar_tensor_tensor", swb + [vbuf_b[ct], cf_b], us[ct][1], out=v3(us[ct][0][:, 0:T]), in0=sw[:, :, 15:W],
                        scalar=cf[:, CF_INVW + ct:CF_INVW + ct + 1], in1=vbs[ct][:, :, 15:W], op0=ALU.mult, op1=ALU.subtract)
                    if grp == "p" and first:
                        tf, tfb = r32.alloc()
                        rc = cf[:, CF_RCFIX + 15 * ct:CF_RCFIX + 15 * ct + 15]
                        dve("tensor_tensor", swb + [cf_b], tfb, out=tf[:, 0:15], in0=sw[:, 0, 15:30], in1=rc, op=ALU.mult)
                        dve("tensor_tensor", tfb + [vbuf_b[ct]], us[ct][1], out=us[ct][0][:, 0:15], in0=tf[:, 0:15], in1=vbs[ct][:, 0, 15:30],
                            op=ALU.subtract)
                pys = []
                for ct in CT:
                    py, pyb = ph()
                    mm([pwb_b] + us[ct][1], [pyb], out=py[:, 0:T], lhsT=pwb[:, ct, :], rhs=us[ct][0][:, 0:T], start=True, stop=True)
                    pys.append((py, pyb))
                sgp = [r32.alloc() for ct in CT]
                for ct in CT:
                    act([ppg[ct][1]], sgp[ct][1], out=sgp[ct][0][:, 0:T], in_=ppg[ct][0][:, 0:T], func=AF.Silu)
                for ct in CT:
                    dve("scalar_tensor_tensor", [pys[ct][1], pcol_b] + sgp[ct][1], [mix_b[6 + ct]], out=mixT[:, 6 + ct, 0:T], in0=pys[ct][0][:, 0:T],
                        scalar=pcol[:, pc + 14 + ct:pc + 15 + ct], in1=sgp[ct][0][:, 0:T], op0=ALU.mult, op1=ALU.mult)
                for ct in CT:
                    if last:
                        deferred.append(lambda ct=ct: pool_state_out(ct, vbs[ct], vbuf_b[ct]))
                    else:
                        pool("tensor_copy", [vbuf_b[ct]], [vbuf_b[ct]], out=vbs[ct][:, :, 0:15], in_=vbs[ct][:, :, Tt:Tt + 15])

            def pool_state_out(ct, vb, vbb):
                if True:
                    ngr = 2 if grp == "s" else 1
                    ns_g = 8 if grp == "s" else 1
                    rows = ns_g * 15
                    for g2 in range(ngr):
                        pa_, pab_ = ph()
                        st, stb = r32.alloc()
                        tmpc, tmpcb = r32.alloc()
                        pool("tensor_copy", [vbb], tmpcb, out=tmpc[:, 0:rows].rearrange("p (s j) -> p s j", j=15),
                             in_=vb[:, g2 * ns_g:(g2 + 1) * ns_g, Tt:Tt + 15])
                        tp(tmpcb + [cf_b], [pab_], out=pa_[0:rows, 0:128], in_=tmpc[:, 0:rows], identity=ident_f)
                        act([pab_], stb, out=st[0:rows, 0:128], in_=pa_[0:rows, 0:128], func=AF.Copy)
                        dst = (npp if grp == "p" else nps)[l, g2 * rows:(g2 + 1) * rows, ct * 128:(ct + 1) * 128]
                        S.dma("sp", dst, st[0:rows, 0:128], stb, [], "dq" + stb[0].buf.name)

            S.tag = f'L{l}{grp}{bi}:front1'
            FS.update(pre if pre is not None else hg_front1(l, T))
            early = (grp == "s" and l == 0 and stage >= 7)
            S.tag = f'L{l}{grp}{bi}:conv'
            convmix_both()
            if early:
                for g_ in (3, 0, 1):
                    load_group(1, g_)
            S.tag = f'L{l}{grp}{bi}:front2'
            front2()
            if nxt is not None:
                for (nti_, nP_) in nxt[1]:
                    rstd_early(nxt[0], nti_, nP_)
            if early:
                load_group(1, 2)
            if prev_wout is not None:
                prev_wout()
            S.tag = f'L{l}{grp}{bi}:conv2'
            convmix_part2()
            S.tag = f'L{l}{grp}{bi}:front3'
            front3()
            qTs = FS['qTs']
            S.tag = f'L{l}{grp}{bi}:pool'
            poolmix_both()
            if early:
                load_group(1, 6)
            vproj()
            if early:
                load_group(1, 4)
            S.tag = f'L{l}{grp}{bi}:middle'
            pk, pkb = pbf()
            pkv = pk.rearrange("p (h t d) -> p h t d", h=4, d=128)
            for h in range(4):
                tok0 = 0
                for tix, (ti, P) in enumerate(tiles):
                    tp([nkh4b[h], cb_b], [pkb], out=pkv[0:P, h, tix, :], in_=nkhv[:, h, tok0:tok0 + P], identity=ident_bf)
                    tok0 += P
            ktv = nkh4.rearrange("p (h t d) -> p h t d", h=4, d=128)
            ktb = nkh4b
            act([pkb], ktb, out=ktv[0:P0, :, 0:ntl, :], in_=pkv[0:P0, :, 0:ntl, :], func=AF.Copy)
            amv = nkb4.rearrange("p (h t s) -> p h t s", h=4, s=128)
            amb = nkb4b
            for h in range(4):
                qT, qTb = qTs[h]
                pA, pAb = ph()
                pAv = pA.rearrange("p (t s) -> p t s", s=128)
                tok0 = 0
                for tix, (ti, P) in enumerate(tiles):
                    mm([nkb4b[h]] + qTb, [pAb], out=pAv[0:P, tix, 0:P], lhsT=nkbv[:, h, tok0:tok0 + P], rhs=qT[:, tok0:tok0 + P],
                       start=True, stop=True)
                    tok0 += P
                dve("tensor_tensor", [pAb, cf_b], [amb[h]], out=amv[0:P0, h, 0:ntl, 0:P0], in0=pAv[0:P0, 0:ntl, 0:P0],
                    in1=negmask.unsqueeze(1).to_broadcast([P0, ntl, P0]), op=ALU.mult)
            S.tag = f'L{l}{grp}{bi}:gates'
            sgs = []

            def gates_and_prenorm():
                for h in range(4):
                    p_g, b_g = proj_fm(C_GB + h * 128, T)
                    sg, sgb = r32.alloc()
                    act([b_g], sgb, out=sg[:, 0:T], in_=p_g[:, 0:T], func=AF.Silu)
                    sgs.append((sg, sgb))
                warm_act_table(AF.Ln)
                if early:
                    load_group(1, 5)

            def prenorm_and_front():
                if nxt is not None:
                    nl, ntiles, nT = nxt
                    if not prenorm_done[0]:
                        tok0 = 0
                        for (ti, P) in ntiles:
                            norm_transpose(nl, ti, P, tok0)
                            tok0 += P
                    S.tag = f'L{l}{grp}{bi}:nfront1'
                    if nl != l:
                        load_weights(nl, with_wout=False)
                        late_wout.append(nl)
                    nxtFS[0] = hg_front1(nl, nT)

            if grp == "p":
                gates_and_prenorm()

            S.tag = f'L{l}{grp}{bi}:stageC'
            pos = [ph() for h in range(4)]
            if grp == "p":
                pus = {}
                pend_evac = []

                def uprime(c):
                    tix = (c * L) // 128
                    c0 = (c * L) % 128
                    pu, pub = pf()
                    puv = pu.rearrange("p (h e) -> p h e", e=128)
                    for h in range(4):
                        mm([ktb[h], vtok_b], [pub], out=puv[:, h, :], lhsT=ktv[c0:c0 + 64, h, tix, :],
                           rhs=vtok[c0:c0 + 64, tix, h * 128:(h + 1) * 128], start=True, stop=True)
                    pus[c] = (puv, pub)

                uprime(0)
                uprime(1)
                for c in range(nch):
                    tix = (c * L) // 128
                    c0 = (c * L) % 128
                    gcx = bi * nch + c
                    if gcx == 0:
                        pool("memset", [], s32_b, ap=s32[:], constant=0.0)
                        pool("memset", [], [sbp_b[0]], ap=sbp[:, :, 0, :], constant=0.0)
                    cur, nx = gcx % 2, (gcx + 1) % 2
                    for h in range(4):
                        qT, qTb = qTs[h]
                        po, pob = pos[h]
                        mm([sbp_b[cur]] + qTb, [pob], out=po[:, c * L:(c + 1) * L], lhsT=sbp[:, h, cur, :], rhs=qT[:, c * L:(c + 1) * L],
                           start=True, stop=False)
                        mm([vtok_b, amb[h]], [pob], out=po[:, c * L:(c + 1) * L], lhsT=vtok[0:P0, tix, h * 128:(h + 1) * 128],
                           rhs=amv[0:P0, h, tix, c0:c0 + L], start=False, stop=True)
                    puv, pub = pus[c]
                    for h in range(4):
                        dve("scalar_tensor_tensor", [s32_b[h], ebl_b, pub], [s32_b[h]], out=s32[:, h, :], in0=s32[:, h, :],
                            scalar=ebl[:, h, c:c + 1], in1=puv[:, h, :], op0=ALU.mult, op1=ALU.subtract)
                    if not (last and c == nch - 1):
                        dve("tensor_copy", s32_b, [sbp_b[nx]], out=sbp[:, :, nx, :], in_=s32[:])
                    if pend_evac:
                        pend_evac.pop(0)()
                    if c + 2 < nch:
                        uprime(c + 2)
                    if nxt is not None and c < len(nxt[1]):
                        nti, nP = nxt[1][c]
                        pe_fn, evac_fn = norm_parts(nxt[0], nti, nP, sum(p_ for (_, p_) in nxt[1][:c]))
                        pe_fn()
                        pend_evac.append(evac_fn)
                        prenorm_done[0] = True
                while pend_evac:
                    pend_evac.pop(0)()
                if last:
                    S.dma("sp", nhp[l].rearrange("h d e -> d h e"), s32[:], s32_b, [], "s32")
            else:
                prep = {}

                def s_load(gj):
                    S.dma("sp", s0st[:, gj % 4, :, :], shg[l, 4 * (gj % 4):4 * (gj % 4) + 4, gj // 4].rearrange("c d e -> d c e"),
                          [], [s0st_b[gj % 4]], f"s0st{gj % 4}")

                def s_prepare(gi):
                    h, g = divmod(gi, 4)
                    slot = gi % 4
                    sb0, sb0b = r32.alloc()
                    sb0v = sb0.bitcast(BF16).rearrange("p (c e) -> p c e", e=128)
                    act([s0st_b[slot]], sb0b, out=sb0v, in_=s0st[:, slot, :, :], func=AF.Copy)
                    km, kmb = r32.alloc()
                    kmv = km.bitcast(BF16).rearrange("p (c d) -> p c d", d=128)
                    dve("tensor_tensor", [ktb[h], cf_b], kmb, out=kmv[0:64, :, :], in0=ktv[0:64, h, 0:1, :].to_broadcast([64, 4, 128]),
                        in1=cf[0:64, CF_SELS + 4 * g:CF_SELS + 4 * g + 4].unsqueeze(2).to_broadcast([64, 4, 128]), op=ALU.mult)
                    prep[gi] = (sb0v, sb0b, kmv, kmb)

                for gj in (0, 1, 2):
                    s_load(gj)
                s_prepare(0)
                for gi in range(16):
                    h, g = divmod(gi, 4)
                    slot = gi % 4
                    qT, qTb = qTs[h]
                    po, pob = pos[h]
                    if gi + 3 < 16:
                        s_load(gi + 3)
                    if gi + 1 < 16:
                        s_prepare(gi + 1)
                    sb0v, sb0b, kmv, kmb = prep.pop(gi)
                    dve("tensor_tensor", [s0st_b[slot], ebl_b], [s0st_b[slot]], out=s0st[:, slot, :, :], in0=s0st[:, slot, :, :],
                        in1=ebl[:, h, 4 * g:4 * g + 4].unsqueeze(2).to_broadcast([128, 4, 128]), op=ALU.mult)
                    pu, pub = pf()
                    puv = pu.rearrange("p (c e) -> p c e", e=128)
                    for cj in range(4):
                        c = 4 * g + cj
                        mm(sb0b + qTb, [pob], out=po[:, c * L:(c + 1) * L], lhsT=sb0v[:, cj, :], rhs=qT[:, c * L:(c + 1) * L],
                           start=True, stop=False)
                        mm([vtok_b, amb[h]], [pob], out=po[:, c * L:(c + 1) * L], lhsT=vtok[0:64, 0, h * 128:(h + 1) * 128],
                           rhs=amv[0:64, h, 0, c * L:(c + 1) * L], start=False, stop=True)
                        mm(kmb + [vtok_b], [pub], out=puv[:, cj, :], lhsT=kmv[0:64, cj, :], rhs=vtok[0:64, 0, h * 128:(h + 1) * 128],
                           start=True, stop=True)
                    dve("tensor_tensor", [s0st_b[slot], pub], [s0st_b[slot]], out=s0st[:, slot, :, :], in0=s0st[:, slot, :, :],
                        in1=puv, op=ALU.subtract)
                    S.dma("pool", nhs[l, 4 * g:4 * g + 4, h].rearrange("c d e -> d c e"), s0st[:, slot, :, :], [s0st_b[slot]], [],
                          f"s0sw{slot}")
            if grp == "s":
                gates_and_prenorm()
            S.tag = f'L{l}{grp}{bi}:stageD'
            for hs in ([range(4)] if grp == "p" else [[0], [1], [2], [3]]):
                osqs = {h: r16.alloc() for h in hs}
                rss = {h: r32.alloc() for h in hs}
                psss = {}
                for h in hs:
                    act([pos[h][1]], osqs[h][1], out=osqs[h][0][:, 0:T], in_=pos[h][0][:, 0:T], func=AF.Square)
                for h in hs:
                    pss, pssb = ph()
                    mm(osqs[h][1] + [cb_b], [pssb], out=pss[:, 0:T], lhsT=ones_bf, rhs=osqs[h][0][:, 0:T], start=True, stop=True)
                    psss[h] = (pss, pssb)
                for h in hs:
                    act([psss[h][1]], rss[h][1], out=rss[h][0][:, 0:T], in_=psss[h][0][:, 0:T], func=AF.Ln, scale=1.0 / 128, bias=EPS)
                for h in hs:
                    act(rss[h][1], rss[h][1], out=rss[h][0][:, 0:T], in_=rss[h][0][:, 0:T], func=AF.Exp, scale=-0.5)
                for h in hs:
                    dve("tensor_tensor", [pos[h][1]] + rss[h][1], rss[h][1], out=rss[h][0][:, 0:T], in0=pos[h][0][:, 0:T],
                        in1=rss[h][0][:, 0:T], op=ALU.mult)
                for h in hs:
                    dve("scalar_tensor_tensor", rss[h][1] + [pcol_b] + sgs[h][1], [mix_b[2 + h]], out=mixT[:, 2 + h, 0:T],
                        in0=rss[h][0][:, 0:T], scalar=pcol[:, pc + 10 + h:pc + 11 + h], in1=sgs[h][0][:, 0:T], op0=ALU.mult, op1=ALU.mult)

            S.tag = f'L{l}{grp}{bi}:wout'
            for fn_ in deferred:
                fn_()
            S.tag = f'L{l}{grp}{bi}:prenorm'
            prenorm_and_front()
            mytag = f'L{l}{grp}{bi}:wout'

            def wout_stage():
                S.tag = mytag
                tok0 = 0
                for (ti, P) in tiles:
                    xb = xres_b[ti]
                    for half in range(2):
                        pO, pOb = pf()
                        for kc in range(8):
                            mm([mix_b[kc], wout_b], [pOb], out=pO[0:P, :], lhsT=mixT[:, kc, tok0:tok0 + P],
                               rhs=wout[:, kc, half * 512:(half + 1) * 512], start=(kc == 0), stop=(kc == 7))
                        xh = xres[0:P, ti, half * 512:(half + 1) * 512]
                        dve("tensor_tensor", [xb, pOb], [xb], out=xh, in0=xh, in1=pO[0:P, :], op=ALU.add)
                    if l == 1 or stage == 6:
                        xt = xres[0:P, ti, :]
                        rstd, smb = rstd_of(xt, xb, P)
                        act([xb] + smb, [xb], out=xt, in_=xt, func=AF.Copy, scale=rstd)
                        pool("tensor_tensor", [xb, gfin_b], [xb], out=xt, in0=xt, in1=gfin[0:P, :], op=ALU.mult)
                        dst = yp[ti * 128:(ti + 1) * 128, :] if grp == "p" else ys[:, :]
                        S.dma("sp", dst, xt, [xb], [], f"xres{ti}")
                    tok0 += P
                for nl_ in late_wout:
                    load_wout(nl_)

            return nxtFS[0], wout_stage

        def load_x(ti):
            if ti < ntile_p:
                S.dma("sp", xres[:, ti, :], xp[ti * 128:(ti + 1) * 128, :], [], [xres_b[ti]], f"xres{ti}")
            elif ti == ntile_p:
                S.dma("sp", xres[0:NSAMP, ntile_p, :], xs[:, :], [], [xres_b[ntile_p]], f"xres{ntile_p}")

        for ti in range(4):
            load_x(ti)
        load_weights(0)
        try:
            seq = []
            for l in range(2):
                for bi in range(nblk):
                    seq.append((l, "p", bi, TB, [(2 * bi, 128), (2 * bi + 1, 128)], 1, TB, LP, bi == nblk - 1))
                seq.append((l, "s", 0, NSAMP, [(ntile_p, NSAMP)], NSEQ_S, TS, TS, True))
            pre = None
            pw_ = None
            for i, a in enumerate(seq):
                l = a[0]
                if l == 1 and a[1] == "p" and a[2] == 0:
                    if stage < 7:
                        raise _Stop()
                    if pre is None:
                        load_weights(1)
                if a[1] == "s" and stage < 6:
                    raise _Stop()
                nxt = None
                if i + 1 < len(seq) and stage >= 7:
                    nxt = (seq[i + 1][0], seq[i + 1][4], seq[i + 1][3])
                if l == 0 and a[1] == "p":
                    for ti in (2 * a[2] + 4, 2 * a[2] + 5):
                        load_x(ti)
                pre, pw_ = block(*a, prenormed=(i > 0 and stage >= 7), nxt=nxt, pre=pre, prev_wout=pw_)
            if pw_ is not None:
                pw_()
        except _Stop:
            pass
        S.flush()
        if S.oplog is not None:
            import json
            json.dump(S.oplog, open(os.environ["KOPLOG"], "w"))
    return nc


_NC_CACHE = {}


def _prep_inputs(inp, nblk=8):
    NP = nblk * TB
    f = lambda a: np.ascontiguousarray(np.asarray(a, dtype=np.float32))
    rows = []
    for l in range(2):
        rows.append(f(inp["conv_w"][l]).reshape(6, 128))
        rows.append(f(inp["hgrn_lb"][l]).reshape(4, 128))
        rows.append(f(inp["hgrn_norm_g"][l]).reshape(4, 128))
        rows.append(f(inp["pool_scale"][l]).reshape(2, 128))
        rows.append(f(inp["norm_g"][l]).reshape(8, 128))
    pp = np.zeros((64, 128), np.float32)
    r = np.concatenate(rows, axis=0)
    pp[:r.shape[0]] = r
    cfc = _make_consts()
    w_in = f(inp["w_in"])
    w_out = f(inp["w_out"])
    pw = f(inp["pool_w"])
    gfin = f(inp["final_norm_g"]).reshape(1, D)
    x_prompt = f(inp["x_prompt"])
    x_sample = f(inp["x_sample"])
    sc = f(inp["state_conv"])
    sh = f(inp["state_hgrn"])
    spo = f(inp["state_pool"])
    maps = []
    for c in range(NCORES):
        s0, s1 = c * NSEQ_S, (c + 1) * NSEQ_S
        maps.append({
            "xp": np.ascontiguousarray(x_prompt[c, :NP]),
            "xs": np.ascontiguousarray(x_sample[s0:s1].reshape(NSAMP, D)),
            "sconv": np.ascontiguousarray(sc[:, s0:s1].reshape(2, 32, 256)),
            "shg": np.ascontiguousarray(sh[:, s0:s1]),
            "spool": np.ascontiguousarray(spo[:, s0:s1].reshape(2, 240, 256)),
            "w_in": w_in, "w_out": w_out, "pp": pp, "pw": pw, "gfin": gfin, "cf": cfc,
        })
    return maps


def _gather(results, nblk=8):
    NP = nblk * TB
    y_prompt = np.stack([r["yp"] for r in results], axis=0).astype(np.float32)
    y_sample = np.concatenate([r["ys"].reshape(NSEQ_S, TS, D) for r in results], axis=0).astype(np.float32)
    new_conv_p = np.stack([r["ncp"] for r in results], axis=1).astype(np.float32)
    new_hgrn_p = np.stack([r["nhp"] for r in results], axis=1).astype(np.float32)
    new_pool_p = np.stack([r["npp"] for r in results], axis=1).astype(np.float32)
    new_conv_s = np.concatenate([r["ncs"].reshape(2, NSEQ_S, 2, 256) for r in results], axis=1).astype(np.float32)
    new_hgrn_s = np.concatenate([r["nhs"] for r in results], axis=1).astype(np.float32)
    new_pool_s = np.concatenate([r["nps"].reshape(2, NSEQ_S, 15, 256) for r in results], axis=1).astype(np.float32)
    return (y_prompt, y_sample, new_conv_p, new_hgrn_p, new_pool_p, new_conv_s, new_hgrn_s, new_pool_s)


def kernel(**inputs):
    nblk = 8
    if nblk not in _NC_CACHE:
        _NC_CACHE[nblk] = build(nblk)
    nc = _NC_CACHE[nblk]
    maps = _prep_inputs(inputs, nblk)
    res = run_bass_kernel_spmd(nc, maps, core_ids=list(range(NCORES)))
    return _gather(res.results, nblk)
```

```python
import numpy as np
from collections import defaultdict
from contextlib import ExitStack
import concourse.bass as bass
import concourse.mybir as mybir
from concourse.bass_utils import run_bass_kernel_spmd

F32 = mybir.dt.float32
BF16 = mybir.dt.bfloat16
AF = mybir.ActivationFunctionType
ALU = mybir.AluOpType

D = 1024
DIN = 3584
EPS = 1e-6
NCORES = 8
TB = 256
LP = 64
NSEQ_S = 16
TS = 4
NSAMP = NSEQ_S * TS
import os
SAME_ENG_SYNC = not bool(os.environ.get("KNOSAME"))
SUB = int(os.environ.get('KSUB', '9'))

C_AH, C_AB, C_AC, C_AG = 0, 256, 512, 768
C_Q, C_F, C_I, C_GB = 1024, 1536, 2048, 2560
C_PV, C_PG = 3072, 3328

CF_IDENT = 0
CF_MASKP = 128
CF_MASKS = 256
CF_SCANP = 320
CF_SCANS = 576
CF_SELS = 640
CF_INVW = 656
CF_RCFIX = 658
NCF = 688


def _make_consts():
    cf = np.zeros((128, NCF), np.float32)
    cf[:, CF_IDENT:CF_IDENT + 128] = np.eye(128, dtype=np.float32)
    s = np.arange(128)[:, None]
    t = np.arange(128)[None, :]
    cf[:, CF_MASKP:CF_MASKP + 128] = -((t >= s) & (s // LP == t // LP)).astype(np.float32)
    s2 = np.arange(64)[:, None]
    t2 = np.arange(64)[None, :]
    cf[:64, CF_MASKS:CF_MASKS + 64] = -((t2 >= s2) & (s2 // TS == t2 // TS)).astype(np.float32)
    cf[:, CF_SCANP:CF_SCANP + 256] = (np.arange(256) % LP != 0).astype(np.float32)[None, :]
    cf[:, CF_SCANS:CF_SCANS + 64] = (np.arange(64) % TS != 0).astype(np.float32)[None, :]
    cf[:64, CF_SELS:CF_SELS + 16] = (np.arange(64)[:, None] // TS == np.arange(16)[None, :]).astype(np.float32)
    p = np.arange(128)
    wins = np.stack([np.where(p < 64, 2.0, 4.0), np.where(p < 64, 8.0, 16.0)], axis=1)
    cf[:, CF_INVW:CF_INVW + 2] = 1.0 / wins
    tt = np.arange(15)[None, None, :] + 1.0
    cf[:, CF_RCFIX:CF_RCFIX + 30] = (1.0 / np.minimum(wins[:, :, None], tt)).reshape(128, 30)
    return cf


class Buf:
    __slots__ = ("name", "w", "r", "rd", "gen", "excl")

    def __init__(self, name, excl=False):
        self.name = name
        self.excl = excl
        self.w = None
        self.r = {}
        self.rd = []
        self.gen = 0


class BH:
    __slots__ = ("buf", "gen", "holder")

    def __init__(self, buf, holder=None):
        holder = buf if holder is None else holder
        holder.gen += 1
        self.holder = holder
        self.buf = buf
        self.gen = holder.gen


class Sched:
    ENGS = ("pe", "act", "dve", "pool", "sp")

    def __init__(self, nc, es):
        self.nc = nc
        self.es = es
        self.engs = {"pe": nc.tensor, "act": nc.scalar, "dve": nc.vector, "pool": nc.gpsimd, "sp": nc.sync}
        self.ops = []
        self.dma_sems = {}
        self.oplog = [] if os.environ.get("KOPLOG") else None
        self.tags = []
        self.tag = ""

    def op(self, eng, fn, kw, reads=(), writes=(), dma_key=None):
        oid = len(self.ops)
        waits = set()
        reads = [self._chk(b) for b in reads]
        writes = [self._chk(b) for b in writes]
        writes = writes + [b for b in reads if b.excl]
        reads = [b for b in reads if not b.excl]
        cand = set()
        for b in reads:
            if b.w is not None:
                cand.add((b.w, False))
        for b in writes:
            if b.w is not None:
                cand.add((b.w, b.excl))
            for x in b.r.values():
                cand.add((x, b.excl))
            for x in b.rd:
                cand.add((x, b.excl))
        for (w, ex) in cand:
            we, wkey = self.ops[w][0], self.ops[w][4]
            if wkey is None and we == eng and dma_key is None:
                if eng == "pe" or ex or not SAME_ENG_SYNC:
                    continue
            waits.add(w)
        waits.discard(oid)
        for w in waits:
            self.ops[w][3] = True
        self.ops.append([eng, (fn, kw), waits, False, dma_key])
        self.tags.append(self.tag)
        for b in reads:
            if dma_key is not None:
                b.rd.append(oid)
            else:
                b.r[eng] = oid
        for b in writes:
            b.w = oid
            b.r = {}
            b.rd = []
        return oid

    @staticmethod
    def _chk(b):
        if isinstance(b, BH):
            assert b.gen == b.holder.gen, f"use of ring slot {b.holder.name} after reallocation"
            return b.buf
        return b

    def dma(self, queue, out, in_, reads, writes, key):
        eng = self.engs[queue]
        return self.op(queue, eng.dma_start, dict(out=out, in_=in_), reads, writes, dma_key=key)

    def flush(self):
        nc, es = self.nc, self.es
        esem = {e: es.enter_context(nc.semaphore("sem_" + e)) for e in self.ENGS}
        cnt = defaultdict(int)
        semobj = {}
        waited = {e: defaultdict(int) for e in self.ENGS}
        comp = [None] * len(self.ops)
        for oid, (e, fn, waits, signal, dma_key) in enumerate(self.ops):
            eng = self.engs[e]
            need = {}
            for w in waits:
                sem, val = comp[w]
                if need.get(sem.num, (None, 0))[1] < val:
                    need[sem.num] = (sem, val)
            dbg = []
            for num, (sem, val) in need.items():
                if waited[e][num] < val:
                    eng.wait_ge(sem, val)
                    waited[e][num] = val
                    dbg.append((sem.name, val))
            if os.environ.get("KDBG"):
                print(oid, e, getattr(fn[0], "__name__", "?"), "waits", dbg, "signal", signal, "dma", dma_key)
            ins = fn[0](**fn[1])
            if self.oplog is not None:
                try:
                    self.oplog.append((oid, e, getattr(fn[0], "__name__", "?"), ins.ins.name, sorted(waits), dma_key, self.tags[oid]))
                except Exception:
                    pass
            if dma_key is not None:
                if dma_key not in self.dma_sems:
                    self.dma_sems[dma_key] = es.enter_context(nc.semaphore("dq_" + dma_key))
                sem = self.dma_sems[dma_key]
                cnt[sem.num] += 16
                semobj[sem.num] = sem
                ins.then_inc(sem, 16)
                comp[oid] = (sem, cnt[sem.num])
            elif signal:
                sem = esem[e]
                cnt[sem.num] += 1
                ins.then_inc(sem, 1)
                comp[oid] = (sem, cnt[sem.num])
        for num, sem in semobj.items():
            nc.sync.wait_ge(sem, cnt[num])


class Ring:
    def __init__(self, tensor, nslots, slot_elems, name):
        self.t = tensor
        self.n = nslots
        self.se = slot_elems
        self.bufs = [Buf(f"{name}{i}") for i in range(nslots)]
        self.pos = 0

    def alloc(self, k=1):
        if self.pos + k > self.n:
            self.pos = 0
        i = self.pos
        self.pos += k
        ap = self.t[:, i * self.se:(i + k) * self.se]
        return ap, [BH(b) for b in self.bufs[i:i + k]]


class _Stop(Exception):
    pass


def build(nblk=8, stage=99):
    NP = nblk * TB
    nc = bass.Bass("TRN2", target_bir_lowering=False)
    dt_in = lambda n, s: nc.dram_tensor(n, s, F32, kind="ExternalInput").ap()
    dt_out = lambda n, s: nc.dram_tensor(n, s, F32, kind="ExternalOutput").ap()
    xp = dt_in("xp", [NP, D])
    xs = dt_in("xs", [NSAMP, D])
    sconv = dt_in("sconv", [2, 32, 256])
    shg = dt_in("shg", [2, NSEQ_S, 4, 128, 128])
    spool = dt_in("spool", [2, 240, 256])
    w_in = dt_in("w_in", [2, D, DIN])
    w_out = dt_in("w_out", [2, D, D])
    pp = dt_in("pp", [64, 128])
    pw = dt_in("pw", [2, 4, 64, 64])
    gfin_d = dt_in("gfin", [1, D])
    cf_d = dt_in("cf", [128, NCF])
    yp = dt_out("yp", [NP, D])
    ys = dt_out("ys", [NSAMP, D])
    ncp = dt_out("ncp", [2, 2, 256])
    nhp = dt_out("nhp", [2, 4, 128, 128])
    npp = dt_out("npp", [2, 15, 256])
    ncs = dt_out("ncs", [2, 32, 256])
    nhs = dt_out("nhs", [2, NSEQ_S, 4, 128, 128])
    nps = dt_out("nps", [2, 240, 256])

    es = ExitStack()
    with es:
        S = Sched(nc, es)
        sb = lambda n, s, d=F32: es.enter_context(nc.sbuf_tensor(n, s, d))
        ntile_p = NP // 128
        NT = ntile_p + 1
        xres = sb("xres", [128, NT, D])
        xres_b = [Buf(f"xres{i}") for i in range(NT)]
        win = [sb(f"win{g}", [128, 8, 512], BF16) for g in range(7)]
        win_b = [Buf(f"win{g}") for g in range(7)]
        wout = sb("wout", [128, 8, D], BF16)
        wout_b = Buf("wout")
        hT = sb("hT", [128, 8, TB], BF16)
        hT_b = Buf("hT")
        mixT = sb("mixT", [128, 8, TB], BF16)
        mix_b = [Buf(f"mix{k}") for k in range(8)]
        hbf = sb("hbf", [128, D], BF16)
        hbf_b = Buf("hbf")
        gfin = sb("gfin_sb", [128, D])
        gfin_b = Buf("gfin")
        cf = sb("cf_sb", [128, NCF])
        cf_b = Buf("cf")
        cb = sb("cb_sb", [128, 256], BF16)
        cb_b = Buf("cb")
        ident_bf = cb[:, 0:128]
        ones_bf = cb[:, 128:256]
        ppt = sb("ppt", [64, 128])
        ppt_b = Buf("ppt")
        pcol = sb("pcol", [128, 64])
        pcol_b = Buf("pcol")
        lbcol = sb("lbcol", [128, 8])
        lb_b = Buf("lbcol")
        small = sb("small", [128, 16])
        small_b = Buf("small")
        pwf = sb("pwf", [128, 2, 128])
        pwf_b = Buf("pwf")
        pwb = sb("pwb", [128, 2, 128], BF16)
        pwb_b = Buf("pwb")
        ubuf = [sb(f"ubuf{c}", [128, 260]) for c in range(2)]
        ubuf_b = [Buf(f"ubuf{c}") for c in range(2)]
        vbuf = [sb(f"vbuf{c}", [128, 304]) for c in range(2)]
        vbuf_b = [Buf(f"vbuf{c}") for c in range(2)]
        s32 = sb("s32", [128, 4, 128])
        s32_b = [Buf(f"s32_{h}") for h in range(4)]
        sbp = sb("sbp", [128, 4, 2, 128], BF16)
        sbp_b = [Buf(f"sbp{i}") for i in range(2)]
        s0st = sb("s0st", [128, 4, 4, 128])
        s0st_b = [Buf(f"s0st{i}") for i in range(4)]
        ebl = sb("ebl", [128, 4, 16])
        ebl_b = Buf("ebl")
        R32 = 19
        R16 = 20
        r32t = sb("r32", [128, R32 * 256])
        r16t = sb("r16", [128, R16 * 256], BF16)
        r32 = Ring(r32t, R32, 256, "r32_")
        r16 = Ring(r16t, R16, 256, "r16_")
        pst = [es.enter_context(nc.psum_tensor(f"psb{i}", [128, 512], F32)) for i in range(7)]
        psbf = es.enter_context(nc.psum_tensor("psbf", [128, 1024], BF16))
        bank_b = [Buf(f"psbank{i}", excl=True) for i in range(8)]
        ph_slots = []
        for j in range(2):
            for i in range(5):
                ph_slots.append((pst[i][:, j * 256:(j + 1) * 256], bank_b[i]))
        ph_pos = [0]

        ph_hold = [Buf(f"phslot{i}") for i in range(10)]

        def ph():
            i = ph_pos[0]
            ph_pos[0] = (i + 1) % len(ph_slots)
            return ph_slots[i][0], BH(ph_slots[i][1], ph_hold[i])

        pf_slots = [(pst[5], bank_b[5]), (pst[6], bank_b[6])]
        pf_pos = [0]

        def pf():
            i = pf_pos[0]
            pf_pos[0] = (i + 1) % 2
            return pf_slots[i]

        def pbf():
            return psbf, bank_b[7]

        V, A, G, PE = nc.vector, nc.scalar, nc.gpsimd, nc.tensor

        def dve(name, reads, writes, **kw):
            S.op("dve", getattr(V, name), kw, reads, writes)

        def act(reads, writes, **kw):
            S.op("act", A.activation, kw, reads, writes)

        def pool(name, reads, writes, **kw):
            S.op("pool", getattr(G, name), kw, reads, writes)

        def mm(reads, writes, **kw):
            S.op("pe", PE.matmul, kw, reads, writes)

        def tp(reads, writes, **kw):
            S.op("pe", PE.transpose, kw, reads, writes)

        vtok = sb("vtok", [128, 2, 512], BF16)
        vtok_b = Buf("vtok")

        S.dma("sp", cf[:], cf_d[:, :], [], [cf_b], "cf")
        S.dma("sp", ppt[:], pp[:, :], [], [ppt_b], "ppt")
        S.dma("sp", gfin[:], gfin_d.partition_broadcast(128), [], [gfin_b], "gfin")
        ident_f = cf[:, CF_IDENT:CF_IDENT + 128]
        pool("tensor_copy", [cf_b], [cb_b], out=ident_bf, in_=ident_f)
        pool("memset", [], [cb_b], ap=ones_bf, constant=1.0)
        pa, pab = ph()
        tp([ppt_b, cf_b], [pab], out=pa[:, 0:64], in_=ppt[:, :], identity=ident_f[0:64, 0:64])
        dve("tensor_copy", [pab], [pcol_b], out=pcol[:], in_=pa[:, 0:64])
        dve("tensor_tensor", [pcol_b], [small_b], out=small[:, 0:4], in0=pcol[:, 6:10], in1=pcol[:, 30:34], op=ALU.subtract)
        act([small_b], [small_b], out=small[:, 4:8], in_=small[:, 0:4], func=AF.Exp)
        dve("tensor_scalar", [small_b], [small_b], out=small[:, 8:12], in0=small[:, 4:8], scalar1=1.0, scalar2=None, op0=ALU.add)
        dve("reciprocal", [small_b], [lb_b], out=lbcol[:, 0:4], in_=small[:, 8:12])

        def load_wout(l):
            S.dma("pool", wout[:], w_out[l].rearrange("(k p) c -> p k c", p=128), [], [wout_b], "wout")

        reloaded = set()

        def load_group(l, g):
            wv = w_in[l].rearrange("(k p) c -> p k c", p=128)
            S.dma("pool", win[g][:], wv[:, :, g * 512:(g + 1) * 512], [], [win_b[g]], f"win{g}")
            reloaded.add((l, g))

        def load_weights(l, with_wout=True):
            pool("memset", [], [pwf_b], ap=pwf[:], constant=0.0)
            for g in (3, 0, 1, 2, 4, 6, 5):
                if (l, g) not in reloaded:
                    load_group(l, g)
            if with_wout:
                load_wout(l)
            for g in range(4):
                ct, hf = g // 2, g % 2
                S.dma("sp", pwf[hf * 64:(hf + 1) * 64, ct, hf * 64:(hf + 1) * 64], pw[l, g], [], [pwf_b], "pwf")
            pool("tensor_copy", [pwf_b], [pwb_b], out=pwb[:], in_=pwf[:])

        dmy = sb("dmy", [128, 4])
        dmy_b = Buf("dmy")
        S.op("pool", G.memset, dict(ap=dmy[:], constant=1.0), [], [dmy_b])

        def warm_act_table(func):
            act([dmy_b], [dmy_b], out=dmy[:, 0:1], in_=dmy[:, 1:2], func=func)

        smt = sb("smt", [128, 32])
        smt_b = [Buf(f"smt{i}") for i in range(8)]
        sm_pos = [0]
        pre_rstd = {}

        def rstd_early(l, ti, P):
            pre_rstd[(l, ti)] = rstd_of(xres[0:P, ti, :], xres_b[ti], P)

        def rstd_of(xt, xb, P, junk=None):
            k_ = sm_pos[0]
            sm_pos[0] = (k_ + 1) % 8
            sm, smb = smt[:, 4 * k_:4 * k_ + 4], [smt_b[k_]]
            if junk is None:
                junk_ap, junk_b = hbf[0:P, :], [hbf_b]
            else:
                junk_ap, junk_b = junk[0][0:P, :], junk[1]
            act([xb], junk_b + [smb[0]], out=junk_ap, in_=xt, func=AF.Square, accum_out=sm[0:P, 0:1])
            act(smb, smb, out=sm[0:P, 1:2], in_=sm[0:P, 0:1], func=AF.Ln, scale=1.0 / D, bias=EPS)
            act(smb, smb, out=sm[0:P, 2:3], in_=sm[0:P, 1:2], func=AF.Exp, scale=-0.5)
            return sm[0:P, 2:3], smb

        def norm_parts(l, ti, P, tok0):
            xt = xres[0:P, ti, :]
            xb = xres_b[ti]
            if (l, ti) in pre_rstd:
                rstd, smb = pre_rstd.pop((l, ti))
            else:
                rstd, smb = rstd_of(xt, xb, P)
            act([xb] + smb, [hbf_b], out=hbf[0:P, :], in_=xt, func=AF.Copy, scale=rstd)
            gc = pcol[:, 24 * l + 16:24 * l + 24]
            st_ = {}

            def pe_fn():
                pt, ptb = pbf()
                ptv = pt.rearrange("p (k t) -> p k t", t=128)
                for kc in range(8):
                    tp([hbf_b, cb_b], [ptb], out=ptv[:, kc, 0:P], in_=hbf[0:P, kc * 128:(kc + 1) * 128], identity=ident_bf[0:P, 0:P])
                st_.update(ptv=ptv, ptb=ptb)

            def evac_fn():
                dve("tensor_tensor", [st_["ptb"], pcol_b], [hT_b], out=hT[:, :, tok0:tok0 + P], in0=st_["ptv"][:, :, 0:P],
                    in1=gc.unsqueeze(2).to_broadcast([128, 8, P]), op=ALU.mult)

            return pe_fn, evac_fn

        def norm_transpose(l, ti, P, tok0):
            pe_fn, evac_fn = norm_parts(l, ti, P, tok0)
            pe_fn()
            evac_fn()

        def proj_fm(col, T):
            g, off = col // 512, col % 512
            ps, psb = ph()
            for kc in range(8):
                mm([win_b[g], hT_b], [psb], out=ps[:, 0:T], lhsT=win[g][:, kc, off:off + 128], rhs=hT[:, kc, 0:T],
                   start=(kc == 0), stop=(kc == 7))
            return ps, psb

        def hg_front1(l, T):
            HR = range(4)
            FS = {}
            pfs = [proj_fm(C_F + h * 128, T) for h in HR]
            r32.pos = 0
            A4, A4b = r32.alloc(4)
            B4, B4b = r32.alloc(4)
            As = [(A4[:, h * 256:(h + 1) * 256], [A4b[h]]) for h in HR]
            Bs = [(B4[:, h * 256:(h + 1) * 256], [B4b[h]]) for h in HR]
            A4v = A4.rearrange("p (h t) -> p h t", h=4)[:, :, 0:T]
            B4v = B4.rearrange("p (h t) -> p h t", h=4)[:, :, 0:T]
            for h in HR:
                act([pfs[h][1]], As[h][1], out=As[h][0][:, 0:T], in_=pfs[h][0][:, 0:T], func=AF.Exp, scale=-1.0)
            act(A4b, B4b, out=B4v, in_=A4v, func=AF.Ln, bias=1.0)
            X4 = {"A": (A4v, A4b, A4), "B": (B4v, B4b, B4)}
            Ds = As if l == 0 else Bs
            FS.update(pfs=pfs, As=As, Bs=Bs, Cs=None, Ds=Ds, X4=X4, D4=X4["A" if l == 0 else "B"], F4=X4["B" if l == 0 else "A"])
            return FS


        def block(l, grp, bi, T, tiles, nseq, Tt, L, last, prenormed, nxt, pre, prev_wout):
            nch = T // L
            P0 = tiles[0][1]
            ntl = len(tiles)
            pc = 24 * l
            first = (bi == 0)
            if grp == "p":
                scanmask = cf[:, CF_SCANP:CF_SCANP + T]
                negmask = cf[0:P0, CF_MASKP:CF_MASKP + P0]
            else:
                scanmask = cf[:, CF_SCANS:CF_SCANS + T]
                negmask = cf[0:P0, CF_MASKS:CF_MASKS + P0]
            if stage < 1:
                raise _Stop()
            if not prenormed:
                tok0 = 0
                for (ti, P) in tiles:
                    norm_transpose(l, ti, P, tok0)
                    tok0 += P
            if stage < 2:
                raise _Stop()

            def v3(ap):
                return ap.rearrange("p (s t) -> p s t", s=nseq)

            nkh4, nkh4b = r16.alloc(4)
            nkb4, nkb4b = r16.alloc(4)
            nkhv = nkh4.rearrange("p (h t) -> p h t", h=4)
            nkbv = nkb4.rearrange("p (h t) -> p h t", h=4)
            HR = range(4)
            FS = {}
            deferred = []
            nxtFS = [None]
            late_wout = []
            prenorm_done = [False]

            def front2():
                pfs, As, Bs, Ds = FS['pfs'], FS['As'], FS['Bs'], FS['Ds']
                A4v, A4b, _ = FS['X4']["A"]
                B4v, B4b, _ = FS['X4']["B"]
                D4v, D4b, _ = FS['D4']
                F4v, F4b, _ = FS['F4']
                C4, C4b = r32.alloc(4)
                Cs = [(C4[:, h * 256:(h + 1) * 256], [C4b[h]]) for h in HR]
                C4v = C4.rearrange("p (h t) -> p h t", h=4)[:, :, 0:T]
                FS['Cs'] = Cs
                FS['C4'] = (C4v, C4b, C4)
                if l == 0:
                    for h in HR:
                        dve("tensor_tensor_scan", Bs[h][1] + [cf_b], Cs[h][1], out=Cs[h][0][:, 0:T], data0=scanmask, data1=Bs[h][0][:, 0:T],
                            initial=0.0, op0=ALU.mult, op1=ALU.subtract)
                    act(B4b, B4b, out=B4v, in_=B4v, func=AF.Exp, scale=-1.0)
                else:
                    for h in HR:
                        act(As[h][1] + [lb_b], As[h][1], out=As[h][0][:, 0:T], in_=As[h][0][:, 0:T], func=AF.Ln, scale=lbcol[:, h:h + 1],
                            bias=1.0)
                    dve("tensor_tensor", A4b + B4b, A4b, out=A4v, in0=A4v, in1=B4v, op=ALU.subtract)
                    for h in HR:
                        dve("tensor_tensor_scan", As[h][1] + [cf_b], Cs[h][1], out=Cs[h][0][:, 0:T], data0=scanmask, data1=As[h][0][:, 0:T],
                            initial=0.0, op0=ALU.mult, op1=ALU.add)
                    act(A4b, A4b, out=A4v, in_=A4v, func=AF.Exp)
                pqs = [proj_fm(C_Q + h * 128, T) for h in HR]
                act(C4b, D4b, out=D4v, in_=C4v, func=AF.Exp)
                act(C4b, C4b, out=C4v, in_=C4v, func=AF.Exp, scale=-1.0)
                FS.update(pqs=pqs)

            def front3():
                Ds, pqs = FS['Ds'], FS['pqs']
                D4v, D4b, D4 = FS['D4']
                F4v, F4b, F4 = FS['F4']
                C4v, C4b, C4 = FS['C4']
                qTs = []
                for h in HR:
                    qT, qTb = r16.alloc()
                    dve("tensor_tensor", [pqs[h][1]] + Ds[h][1], qTb, out=qT[:, 0:T], in0=pqs[h][0][:, 0:T], in1=Ds[h][0][:, 0:T], op=ALU.mult)
                    qTs.append((qT, qTb))
                dve("scalar_tensor_tensor", F4b + C4b, nkb4b, out=nkbv[:, :, 0:T], in0=F4v, scalar=1.0, in1=C4v,
                    op0=ALU.subtract, op1=ALU.mult)
                act(D4b, [ebl_b], out=ebl[:, :, 0:nch], in_=D4.rearrange("p (h t) -> p h t", h=4)[:, :, L - 1:T:L], func=AF.Copy)
                dve("tensor_tensor", nkb4b + [ebl_b], nkh4b, out=nkhv[:, :, 0:T].rearrange("p h (c l) -> p h c l", l=L),
                    in0=nkbv[:, :, 0:T].rearrange("p h (c l) -> p h c l", l=L),
                    in1=ebl[:, :, 0:nch].unsqueeze(3).to_broadcast([128, 4, nch, L]), op=ALU.mult)
                tok0 = 0
                for tix, (ti, P) in enumerate(tiles):
                    pv_, pvb_ = pf()
                    for kc in range(8):
                        mm([hT_b, win_b[4]], [pvb_], out=pv_[0:P, :], lhsT=hT[:, kc, tok0:tok0 + P], rhs=win[4][:, kc, :],
                           start=(kc == 0), stop=(kc == 7))
                    act([pvb_], [vtok_b], out=vtok[0:P, tix, :], in_=pv_[0:P, :], func=AF.Copy)
                    tok0 += P

                FS.update(qTs=qTs)

            def convmix_both():
                W = Tt + 2
                CT = range(2)
                ubs = [ubuf[ct][:, 0:nseq * W].rearrange("p (s w) -> p s w", w=W) for ct in CT]
                cw = lambda k, ct: pcol[:, pc + 2 * k + ct:pc + 2 * k + ct + 1]
                if first:
                    if grp == "p":
                        for ct in CT:
                            pool("memset", [], [ubuf_b[ct]], ap=ubs[ct][:, :, 0:2], constant=0.0)
                    else:
                        scv, scvb = r32.alloc()
                        S.dma("sp", scv[0:32, :], sconv[l], [], scvb, "dq" + scvb[0].buf.name)
                        for ct in CT:
                            pa_, pab_ = ph()
                            tp(scvb + [cf_b], [pab_], out=pa_[:, 0:32], in_=scv[0:32, ct * 128:(ct + 1) * 128], identity=ident_f[0:32, 0:32])
                            act([pab_], [ubuf_b[ct]], out=ubs[ct][:, :, 0:2], in_=pa_[:, 0:32].rearrange("p (s j) -> p s j", j=2),
                                func=AF.Copy)
                pah = [proj_fm(C_AH + ct * 128, T) for ct in CT]
                pac = [proj_fm(C_AC + ct * 128, T) for ct in CT]
                ahs = [r32.alloc() for ct in CT]
                for ct in CT:
                    act([pah[ct][1]], ahs[ct][1], out=ahs[ct][0][:, 0:T], in_=pah[ct][0][:, 0:T], func=AF.Copy)
                pag = [proj_fm(C_AG + ct * 128, T) for ct in CT]
                for ct in CT:
                    dve("tensor_tensor", [pac[ct][1]] + ahs[ct][1], [ubuf_b[ct]], out=ubs[ct][:, :, 2:2 + Tt], in0=v3(pac[ct][0][:, 0:T]),
                        in1=v3(ahs[ct][0][:, 0:T]), op=ALU.mult)
                y1s = [r32.alloc() for ct in CT]
                for ct in CT:
                    act([ubuf_b[ct], pcol_b], y1s[ct][1], out=v3(y1s[ct][0][:, 0:T]), in_=ubs[ct][:, :, 0:Tt], func=AF.Copy, scale=cw(0, ct))
                pab2 = [proj_fm(C_AB + ct * 128, T) for ct in CT]
                for k in (1, 2):
                    for ct in CT:
                        dve("scalar_tensor_tensor", [ubuf_b[ct], pcol_b] + y1s[ct][1], y1s[ct][1], out=v3(y1s[ct][0][:, 0:T]),
                            in0=ubs[ct][:, :, k:k + Tt], scalar=cw(k, ct), in1=v3(y1s[ct][0][:, 0:T]), op0=ALU.mult, op1=ALU.add)
                sas = [r32.alloc() for ct in CT]
                for ct in CT:
                    act([pag[ct][1]], sas[ct][1], out=sas[ct][0][:, 0:T], in_=pag[ct][0][:, 0:T], func=AF.Silu)
                for ct in CT:
                    dve("tensor_tensor", [pab2[ct][1]] + y1s[ct][1], y1s[ct][1], out=y1s[ct][0][:, 0:T], in0=pab2[ct][0][:, 0:T],
                        in1=y1s[ct][0][:, 0:T], op=ALU.mult)
                warm_act_table(AF.Exp)
                FS["conv2"] = (CT, y1s, sas, ubs)

            def convmix_part2():
                CT, y1s, sas, ubs = FS["conv2"]
                for ct in CT:
                    dve("tensor_tensor", y1s[ct][1] + sas[ct][1], [mix_b[ct]], out=mixT[:, ct, 0:T], in0=y1s[ct][0][:, 0:T],
                        in1=sas[ct][0][:, 0:T], op=ALU.mult)
                for ct in CT:
                    if last:
                        deferred.append(lambda ct=ct: conv_state_out(ct, ubs[ct], ubuf_b[ct]))
                    else:
                        pool("tensor_copy", [ubuf_b[ct]], [ubuf_b[ct]], out=ubs[ct][:, :, 0:2], in_=ubs[ct][:, :, Tt:Tt + 2])

            def conv_state_out(ct, ub, ubb):
                if True:
                    rows = nseq * 2
                    pa_, pab_ = ph()
                    st, stb = r32.alloc()
                    tmpc, tmpcb = r32.alloc()
                    pool("tensor_copy", [ubb], tmpcb, out=tmpc[:, 0:rows].rearrange("p (s j) -> p s j", j=2), in_=ub[:, :, Tt:Tt + 2])
                    tp(tmpcb + [cf_b], [pab_], out=pa_[0:rows, 0:128], in_=tmpc[:, 0:rows], identity=ident_f)
                    act([pab_], stb, out=st[0:rows, 0:128], in_=pa_[0:rows, 0:128], func=AF.Copy)
                    dst = (ncp if grp == "p" else ncs)[l, :, ct * 128:(ct + 1) * 128]
                    S.dma("sp", dst, st[0:rows, 0:128], stb, [], "dq" + stb[0].buf.name)

            def poolmix_both():
                W = Tt + 15
                CT = range(2)
                vbs = [vbuf[ct][:, 0:nseq * W].rearrange("p (s w) -> p s w", w=W) for ct in CT]
                if first:
                    if grp == "p":
                        for ct in CT:
                            pool("memset", [], [vbuf_b[ct]], ap=vbs[ct][:, :, 0:15], constant=0.0)
                    else:
                        for g2 in range(2):
                            spl_, splb_ = r32.alloc()
                            S.dma("sp", spl_[0:120, :], spool[l, g2 * 120:(g2 + 1) * 120, :], [], splb_, "dq" + splb_[0].buf.name)
                            for ct in CT:
                                pa_, pab_ = ph()
                                tp(splb_ + [cf_b], [pab_], out=pa_[:, 0:120], in_=spl_[0:120, ct * 128:(ct + 1) * 128],
                                   identity=ident_f[0:120, 0:120])
                                act([pab_], [vbuf_b[ct]], out=vbs[ct][:, g2 * 8:(g2 + 1) * 8, 0:15],
                                    in_=pa_[:, 0:120].rearrange("p (s j) -> p s j", j=15), func=AF.Copy)
                ppv = [proj_fm(C_PV + ct * 128, T) for ct in CT]
                for ct in CT:
                    act([ppv[ct][1]], [vbuf_b[ct]], out=vbs[ct][:, :, 15:15 + Tt], in_=v3(ppv[ct][0][:, 0:T]), func=AF.Copy)
                ppg = [proj_fm(C_PG + ct * 128, T) for ct in CT]

                def warr():
                    a_, b_ = r32.alloc(2)
                    return a_[:, 0:nseq * W].rearrange("p (s w) -> p s w", w=W), b_

                s2 = [warr() for ct in CT]
                s4 = [warr() for ct in CT]
                for ct in CT:
                    dve("tensor_tensor", [vbuf_b[ct]], s2[ct][1], out=s2[ct][0][:, :, 1:W], in0=vbs[ct][:, :, 1:W], in1=vbs[ct][:, :, 0:W - 1],
                        op=ALU.add)
                a2, a2b = s2[0]
                sw0, sw0b = s4[0]
                dve("tensor_tensor", a2b, sw0b, out=sw0[64:128, :, 3:W], in0=a2[64:128, :, 3:W], in1=a2[64:128, :, 1:W - 2], op=ALU.add)
                b2, b2b = s2[1]
                b4, b4b = s4[1]
                dve("tensor_tensor", b2b, b4b, out=b4[:, :, 3:W], in0=b2[:, :, 3:W], in1=b2[:, :, 1:W - 2], op=ALU.add)
                act(a2b, sw0b, out=sw0[0:64, :, 15:W], in_=a2[0:64, :, 15:W], func=AF.Copy)
                b8, b8b = b2, b2b
                dve("tensor_tensor", b4b, b8b, out=b8[:, :, 7:W], in0=b4[:, :, 7:W], in1=b4[:, :, 3:W - 4], op=ALU.add)
                sw1, sw1b = b4, b4b
                dve("tensor_tensor", b8b, sw1b, out=sw1[64:128, :, 15:W], in0=b8[64:128, :, 15:W], in1=b8[64:128, :, 7:W - 8], op=ALU.add)
                act(b8b, sw1b, out=sw1[0:64, :, 15:W], in_=b8[0:64, :, 15:W], func=AF.Copy)
                sws = [(sw0, sw0b), (sw1, sw1b)]
                us = [r16.alloc() for ct in CT]
                for ct in CT:
                    sw, swb = sws[ct]
                    dve("scalar_tensor_tensor", swb + [vbuf_b[ct], cf_b], us[ct][1], out=v3(us[ct][0][:, 0:T]), in0=sw[:, :, 15:W],
                        scalar=cf[:, CF_INVW + ct:CF_INVW + ct + 1], in1=vbs[ct][:, :, 15:W], op0=ALU.mult, op1=ALU.subtract)
                    if grp == "p" and first:
                        tf, tfb = r32.alloc()
                        rc = cf[:, CF_RCFIX + 15 * ct:CF_RCFIX + 15 * ct + 15]
                        dve("tensor_tensor", swb + [cf_b], tfb, out=tf[:, 0:15], in0=sw[:, 0, 15:30], in1=rc, op=ALU.mult)
                        dve("tensor_tensor", tfb + [vbuf_b[ct]], us[ct][1], out=us[ct][0][:, 0:15], in0=tf[:, 0:15], in1=vbs[ct][:, 0, 15:30],
                            op=ALU.subtract)
                pys = []
                for ct in CT:
                    py, pyb = ph()
                    mm([pwb_b] + us[ct][1], [pyb], out=py[:, 0:T], lhsT=pwb[:, ct, :], rhs=us[ct][0][:, 0:T], start=True, stop=True)
                    pys.append((py, pyb))
                sgp = [r32.alloc() for ct in CT]
                for ct in CT:
                    act([ppg[ct][1]], sgp[ct][1], out=sgp[ct][0][:, 0:T], in_=ppg[ct][0][:, 0:T], func=AF.Silu)
                for ct in CT:
                    dve("scalar_tensor_tensor", [pys[ct][1], pcol_b] + sgp[ct][1], [mix_b[6 + ct]], out=mixT[:, 6 + ct, 0:T], in0=pys[ct][0][:, 0:T],
                        scalar=pcol[:, pc + 14 + ct:pc + 15 + ct], in1=sgp[ct][0][:, 0:T], op0=ALU.mult, op1=ALU.mult)
                for ct in CT:
                    if last:
                        deferred.append(lambda ct=ct: pool_state_out(ct, vbs[ct], vbuf_b[ct]))
                    else:
                        pool("tensor_copy", [vbuf_b[ct]], [vbuf_b[ct]], out=vbs[ct][:, :, 0:15], in_=vbs[ct][:, :, Tt:Tt + 15])

            def pool_state_out(ct, vb, vbb):
                if True:
                    ngr = 2 if grp == "s" else 1
                    ns_g = 8 if grp == "s" else 1
                    rows = ns_g * 15
                    for g2 in range(ngr):
                        pa_, pab_ = ph()
                        st, stb = r32.alloc()
                        tmpc, tmpcb = r32.alloc()
                        pool("tensor_copy", [vbb], tmpcb, out=tmpc[:, 0:rows].rearrange("p (s j) -> p s j", j=15),
                             in_=vb[:, g2 * ns_g:(g2 + 1) * ns_g, Tt:Tt + 15])
                        tp(tmpcb + [cf_b], [pab_], out=pa_[0:rows, 0:128], in_=tmpc[:, 0:rows], identity=ident_f)
                        act([pab_], stb, out=st[0:rows, 0:128], in_=pa_[0:rows, 0:128], func=AF.Copy)
                        dst = (npp if grp == "p" else nps)[l, g2 * rows:(g2 + 1) * rows, ct * 128:(ct + 1) * 128]
                        S.dma("sp", dst, st[0:rows, 0:128], stb, [], "dq" + stb[0].buf.name)

            S.tag = f'L{l}{grp}{bi}:front1'
            FS.update(pre if pre is not None else hg_front1(l, T))
            early = (grp == "s" and l == 0 and stage >= 7)
            S.tag = f'L{l}{grp}{bi}:conv'
            convmix_both()
            if early:
                for g_ in (3, 0, 1):
                    load_group(1, g_)
            S.tag = f'L{l}{grp}{bi}:front2'
            front2()
            if nxt is not None:
                for (nti_, nP_) in nxt[1]:
                    rstd_early(nxt[0], nti_, nP_)
            if early:
                load_group(1, 2)
            if prev_wout is not None:
                prev_wout()
            S.tag = f'L{l}{grp}{bi}:conv2'
            convmix_part2()
            S.tag = f'L{l}{grp}{bi}:front3'
            front3()
            if early:
                load_group(1, 4)
            qTs = FS['qTs']
            S.tag = f'L{l}{grp}{bi}:pool'
            poolmix_both()
            if early:
                load_group(1, 6)
            S.tag = f'L{l}{grp}{bi}:middle'
            pk, pkb = pbf()
            pkv = pk.rearrange("p (h t d) -> p h t d", h=4, d=128)
            for h in range(4):
                tok0 = 0
                for tix, (ti, P) in enumerate(tiles):
                    tp([nkh4b[h], cb_b], [pkb], out=pkv[0:P, h, tix, :], in_=nkhv[:, h, tok0:tok0 + P], identity=ident_bf)
                    tok0 += P
            ktv = nkh4.rearrange("p (h t d) -> p h t d", h=4, d=128)
            ktb = nkh4b
            act([pkb], ktb, out=ktv[0:P0, :, 0:ntl, :], in_=pkv[0:P0, :, 0:ntl, :], func=AF.Copy)
            amv = nkb4.rearrange("p (h t s) -> p h t s", h=4, s=128)
            amb = nkb4b
            for h in range(4):
                qT, qTb = qTs[h]
                pA, pAb = ph()
                pAv = pA.rearrange("p (t s) -> p t s", s=128)
                tok0 = 0
                for tix, (ti, P) in enumerate(tiles):
                    mm([nkb4b[h]] + qTb, [pAb], out=pAv[0:P, tix, 0:P], lhsT=nkbv[:, h, tok0:tok0 + P], rhs=qT[:, tok0:tok0 + P],
                       start=True, stop=True)
                    tok0 += P
                dve("tensor_tensor", [pAb, cf_b], [amb[h]], out=amv[0:P0, h, 0:ntl, 0:P0], in0=pAv[0:P0, 0:ntl, 0:P0],
                    in1=negmask.unsqueeze(1).to_broadcast([P0, ntl, P0]), op=ALU.mult)
            S.tag = f'L{l}{grp}{bi}:gates'
            sgs = []

            def gates_and_prenorm():
                for h in range(4):
                    p_g, b_g = proj_fm(C_GB + h * 128, T)
                    sg, sgb = r32.alloc()
                    act([b_g], sgb, out=sg[:, 0:T], in_=p_g[:, 0:T], func=AF.Silu)
                    sgs.append((sg, sgb))
                warm_act_table(AF.Ln)
                if early:
                    load_group(1, 5)

            def prenorm_and_front():
                if nxt is not None:
                    nl, ntiles, nT = nxt
                    if not prenorm_done[0]:
                        tok0 = 0
                        for (ti, P) in ntiles:
                            norm_transpose(nl, ti, P, tok0)
                            tok0 += P
                    S.tag = f'L{l}{grp}{bi}:nfront1'
                    if nl != l:
                        load_weights(nl, with_wout=False)
                        late_wout.append(nl)
                    nxtFS[0] = hg_front1(nl, nT)

            if grp == "p":
                gates_and_prenorm()

            S.tag = f'L{l}{grp}{bi}:stageC'
            pos = [ph() for h in range(4)]
            if grp == "p":
                pus = {}
                pend_evac = []

                def uprime(c):
                    tix = (c * L) // 128
                    c0 = (c * L) % 128
                    pu, pub = pf()
                    puv = pu.rearrange("p (h e) -> p h e", e=128)
                    for h in range(4):
                        mm([ktb[h], vtok_b], [pub], out=puv[:, h, :], lhsT=ktv[c0:c0 + 64, h, tix, :],
                           rhs=vtok[c0:c0 + 64, tix, h * 128:(h + 1) * 128], start=True, stop=True)
                    pus[c] = (puv, pub)

                uprime(0)
                uprime(1)
                for c in range(nch):
                    tix = (c * L) // 128
                    c0 = (c * L) % 128
                    gcx = bi * nch + c
                    if gcx == 0:
                        pool("memset", [], s32_b, ap=s32[:], constant=0.0)
                        pool("memset", [], [sbp_b[0]], ap=sbp[:, :, 0, :], constant=0.0)
                    cur, nx = gcx % 2, (gcx + 1) % 2
                    for h in range(4):
                        qT, qTb = qTs[h]
                        po, pob = pos[h]
                        mm([sbp_b[cur]] + qTb, [pob], out=po[:, c * L:(c + 1) * L], lhsT=sbp[:, h, cur, :], rhs=qT[:, c * L:(c + 1) * L],
                           start=True, stop=False)
                        mm([vtok_b, amb[h]], [pob], out=po[:, c * L:(c + 1) * L], lhsT=vtok[0:P0, tix, h * 128:(h + 1) * 128],
                           rhs=amv[0:P0, h, tix, c0:c0 + L], start=False, stop=True)
                    puv, pub = pus[c]
                    for h in range(4):
                        dve("scalar_tensor_tensor", [s32_b[h], ebl_b, pub], [s32_b[h]], out=s32[:, h, :], in0=s32[:, h, :],
                            scalar=ebl[:, h, c:c + 1], in1=puv[:, h, :], op0=ALU.mult, op1=ALU.subtract)
                    if not (last and c == nch - 1):
                        dve("tensor_copy", s32_b, [sbp_b[nx]], out=sbp[:, :, nx, :], in_=s32[:])
                    if pend_evac:
                        pend_evac.pop(0)()
                    if c + 2 < nch:
                        uprime(c + 2)
                    if nxt is not None and c < len(nxt[1]):
                        nti, nP = nxt[1][c]
                        pe_fn, evac_fn = norm_parts(nxt[0], nti, nP, sum(p_ for (_, p_) in nxt[1][:c]))
                        pe_fn()
                        pend_evac.append(evac_fn)
                        prenorm_done[0] = True
                while pend_evac:
                    pend_evac.pop(0)()
                if last:
                    S.dma("sp", nhp[l].rearrange("h d e -> d h e"), s32[:], s32_b, [], "s32")
            else:
                prep = {}

                def s_load(gj):
                    S.dma("sp", s0st[:, gj % 4, :, :], shg[l, 4 * (gj % 4):4 * (gj % 4) + 4, gj // 4].rearrange("c d e -> d c e"),
                          [], [s0st_b[gj % 4]], f"s0st{gj % 4}")

                def s_prepare(gi):
                    h, g = divmod(gi, 4)
                    slot = gi % 4
                    sb0, sb0b = r32.alloc()
                    sb0v = sb0.bitcast(BF16).rearrange("p (c e) -> p c e", e=128)
                    act([s0st_b[slot]], sb0b, out=sb0v, in_=s0st[:, slot, :, :], func=AF.Copy)
                    km, kmb = r32.alloc()
                    kmv = km.bitcast(BF16).rearrange("p (c d) -> p c d", d=128)
                    dve("tensor_tensor", [ktb[h], cf_b], kmb, out=kmv[0:64, :, :], in0=ktv[0:64, h, 0:1, :].to_broadcast([64, 4, 128]),
                        in1=cf[0:64, CF_SELS + 4 * g:CF_SELS + 4 * g + 4].unsqueeze(2).to_broadcast([64, 4, 128]), op=ALU.mult)
                    prep[gi] = (sb0v, sb0b, kmv, kmb)

                for gj in (0, 1, 2):
                    s_load(gj)
                s_prepare(0)
                for gi in range(16):
                    h, g = divmod(gi, 4)
                    slot = gi % 4
                    qT, qTb = qTs[h]
                    po, pob = pos[h]
                    if gi + 3 < 16:
                        s_load(gi + 3)
                    if gi + 1 < 16:
                        s_prepare(gi + 1)
                    sb0v, sb0b, kmv, kmb = prep.pop(gi)
                    so, sob = r32.alloc(2)
                    sov = so.rearrange("p (c e) -> p c e", e=128)
                    dve("tensor_tensor", [s0st_b[slot], ebl_b], sob, out=sov, in0=s0st[:, slot, :, :],
                        in1=ebl[:, h, 4 * g:4 * g + 4].unsqueeze(2).to_broadcast([128, 4, 128]), op=ALU.mult)
                    pu, pub = pf()
                    puv = pu.rearrange("p (c e) -> p c e", e=128)
                    for cj in range(4):
                        c = 4 * g + cj
                        mm(sb0b + qTb, [pob], out=po[:, c * L:(c + 1) * L], lhsT=sb0v[:, cj, :], rhs=qT[:, c * L:(c + 1) * L],
                           start=True, stop=False)
                        mm([vtok_b, amb[h]], [pob], out=po[:, c * L:(c + 1) * L], lhsT=vtok[0:64, 0, h * 128:(h + 1) * 128],
                           rhs=amv[0:64, h, 0, c * L:(c + 1) * L], start=False, stop=True)
                        mm(kmb + [vtok_b], [pub], out=puv[:, cj, :], lhsT=kmv[0:64, cj, :], rhs=vtok[0:64, 0, h * 128:(h + 1) * 128],
                           start=True, stop=True)
                    dve("tensor_tensor", sob + [pub], sob, out=sov, in0=sov, in1=puv, op=ALU.subtract)
                    S.dma("pool", nhs[l, 4 * g:4 * g + 4, h].rearrange("c d e -> d c e"), sov, sob, [],
                          "dw" + sob[0].buf.name)
            if grp == "s":
                gates_and_prenorm()
            S.tag = f'L{l}{grp}{bi}:stageD'
            for hs in ([range(4)] if grp == "p" else [[0], [1], [2], [3]]):
                osqs = {h: r16.alloc() for h in hs}
                rss = {h: r32.alloc() for h in hs}
                psss = {}
                for h in hs:
                    act([pos[h][1]], osqs[h][1], out=osqs[h][0][:, 0:T], in_=pos[h][0][:, 0:T], func=AF.Square)
                for h in hs:
                    pss, pssb = ph()
                    mm(osqs[h][1] + [cb_b], [pssb], out=pss[:, 0:T], lhsT=ones_bf, rhs=osqs[h][0][:, 0:T], start=True, stop=True)
                    psss[h] = (pss, pssb)
                for h in hs:
                    act([psss[h][1]], rss[h][1], out=rss[h][0][:, 0:T], in_=psss[h][0][:, 0:T], func=AF.Ln, scale=1.0 / 128, bias=EPS)
                for h in hs:
                    act(rss[h][1], rss[h][1], out=rss[h][0][:, 0:T], in_=rss[h][0][:, 0:T], func=AF.Exp, scale=-0.5)
                for h in hs:
                    dve("tensor_tensor", [pos[h][1]] + rss[h][1], rss[h][1], out=rss[h][0][:, 0:T], in0=pos[h][0][:, 0:T],
                        in1=rss[h][0][:, 0:T], op=ALU.mult)
                for h in hs:
                    dve("scalar_tensor_tensor", rss[h][1] + [pcol_b] + sgs[h][1], [mix_b[2 + h]], out=mixT[:, 2 + h, 0:T],
                        in0=rss[h][0][:, 0:T], scalar=pcol[:, pc + 10 + h:pc + 11 + h], in1=sgs[h][0][:, 0:T], op0=ALU.mult, op1=ALU.mult)

            S.tag = f'L{l}{grp}{bi}:wout'
            for fn_ in deferred:
                fn_()
            S.tag = f'L{l}{grp}{bi}:prenorm'
            prenorm_and_front()
            mytag = f'L{l}{grp}{bi}:wout'

            def wout_stage():
                S.tag = mytag
                tok0 = 0
                for (ti, P) in tiles:
                    xb = xres_b[ti]
                    for half in range(2):
                        pO, pOb = pf()
                        for kc in range(8):
                            mm([mix_b[kc], wout_b], [pOb], out=pO[0:P, :], lhsT=mixT[:, kc, tok0:tok0 + P],
                               rhs=wout[:, kc, half * 512:(half + 1) * 512], start=(kc == 0), stop=(kc == 7))
                        xh = xres[0:P, ti, half * 512:(half + 1) * 512]
                        dve("tensor_tensor", [xb, pOb], [xb], out=xh, in0=xh, in1=pO[0:P, :], op=ALU.add)
                    if l == 1 or stage == 6:
                        xt = xres[0:P, ti, :]
                        rstd, smb = rstd_of(xt, xb, P)
                        act([xb] + smb, [xb], out=xt, in_=xt, func=AF.Copy, scale=rstd)
                        pool("tensor_tensor", [xb, gfin_b], [xb], out=xt, in0=xt, in1=gfin[0:P, :], op=ALU.mult)
                        dst = yp[ti * 128:(ti + 1) * 128, :] if grp == "p" else ys[:, :]
                        S.dma("sp", dst, xt, [xb], [], f"xres{ti}")
                    tok0 += P
                for nl_ in late_wout:
                    load_wout(nl_)

            return nxtFS[0], wout_stage

        def load_x(ti):
            if ti < ntile_p:
                S.dma("sp", xres[:, ti, :], xp[ti * 128:(ti + 1) * 128, :], [], [xres_b[ti]], f"xres{ti}")
            elif ti == ntile_p:
                S.dma("sp", xres[0:NSAMP, ntile_p, :], xs[:, :], [], [xres_b[ntile_p]], f"xres{ntile_p}")

        for ti in range(4):
            load_x(ti)
        load_weights(0)
        try:
            seq = []
            for l in range(2):
                for bi in range(nblk):
                    seq.append((l, "p", bi, TB, [(2 * bi, 128), (2 * bi + 1, 128)], 1, TB, LP, bi == nblk - 1))
                seq.append((l, "s", 0, NSAMP, [(ntile_p, NSAMP)], NSEQ_S, TS, TS, True))
            pre = None
            pw_ = None
            for i, a in enumerate(seq):
                l = a[0]
                if l == 1 and a[1] == "p" and a[2] == 0:
                    if stage < 7:
                        raise _Stop()
                    if pre is None:
                        load_weights(1)
                if a[1] == "s" and stage < 6:
                    raise _Stop()
                nxt = None
                if i + 1 < len(seq) and stage >= 7:
                    nxt = (seq[i + 1][0], seq[i + 1][4], seq[i + 1][3])
                if l == 0 and a[1] == "p":
                    for ti in (2 * a[2] + 4, 2 * a[2] + 5):
                        load_x(ti)
                pre, pw_ = block(*a, prenormed=(i > 0 and stage >= 7), nxt=nxt, pre=pre, prev_wout=pw_)
            if pw_ is not None:
                pw_()
        except _Stop:
            pass
        S.flush()
        if S.oplog is not None:
            import json
            json.dump(S.oplog, open(os.environ["KOPLOG"], "w"))
    return nc


_NC_CACHE = {}


def _prep_inputs(inp, nblk=8):
    NP = nblk * TB
    f = lambda a: np.ascontiguousarray(np.asarray(a, dtype=np.float32))
    rows = []
    for l in range(2):
        rows.append(f(inp["conv_w"][l]).reshape(6, 128))
        rows.append(f(inp["hgrn_lb"][l]).reshape(4, 128))
        rows.append(f(inp["hgrn_norm_g"][l]).reshape(4, 128))
        rows.append(f(inp["pool_scale"][l]).reshape(2, 128))
        rows.append(f(inp["norm_g"][l]).reshape(8, 128))
    pp = np.zeros((64, 128), np.float32)
    r = np.concatenate(rows, axis=0)
    pp[:r.shape[0]] = r
    cfc = _make_consts()
    w_in = f(inp["w_in"])
    w_out = f(inp["w_out"])
    pw = f(inp["pool_w"])
    gfin = f(inp["final_norm_g"]).reshape(1, D)
    x_prompt = f(inp["x_prompt"])
    x_sample = f(inp["x_sample"])
    sc = f(inp["state_conv"])
    sh = f(inp["state_hgrn"])
    spo = f(inp["state_pool"])
    maps = []
    for c in range(NCORES):
        s0, s1 = c * NSEQ_S, (c + 1) * NSEQ_S
        maps.append({
            "xp": np.ascontiguousarray(x_prompt[c, :NP]),
            "xs": np.ascontiguousarray(x_sample[s0:s1].reshape(NSAMP, D)),
            "sconv": np.ascontiguousarray(sc[:, s0:s1].reshape(2, 32, 256)),
            "shg": np.ascontiguousarray(sh[:, s0:s1]),
            "spool": np.ascontiguousarray(spo[:, s0:s1].reshape(2, 240, 256)),
            "w_in": w_in, "w_out": w_out, "pp": pp, "pw": pw, "gfin": gfin, "cf": cfc,
        })
    return maps


def _gather(results, nblk=8):
    NP = nblk * TB
    y_prompt = np.stack([r["yp"] for r in results], axis=0).astype(np.float32)
    y_sample = np.concatenate([r["ys"].reshape(NSEQ_S, TS, D) for r in results], axis=0).astype(np.float32)
    new_conv_p = np.stack([r["ncp"] for r in results], axis=1).astype(np.float32)
    new_hgrn_p = np.stack([r["nhp"] for r in results], axis=1).astype(np.float32)
    new_pool_p = np.stack([r["npp"] for r in results], axis=1).astype(np.float32)
    new_conv_s = np.concatenate([r["ncs"].reshape(2, NSEQ_S, 2, 256) for r in results], axis=1).astype(np.float32)
    new_hgrn_s = np.concatenate([r["nhs"] for r in results], axis=1).astype(np.float32)
    new_pool_s = np.concatenate([r["nps"].reshape(2, NSEQ_S, 15, 256) for r in results], axis=1).astype(np.float32)
    return (y_prompt, y_sample, new_conv_p, new_hgrn_p, new_pool_p, new_conv_s, new_hgrn_s, new_pool_s)


def kernel(**inputs):
    nblk = 8
    if nblk not in _NC_CACHE:
        _NC_CACHE[nblk] = build(nblk)
    nc = _NC_CACHE[nblk]
    maps = _prep_inputs(inputs, nblk)
    res = run_bass_kernel_spmd(nc, maps, core_ids=list(range(NCORES)))
    return _gather(res.results, nblk)
```
